# Optimizing a Trainium2 kernel written in Bass

```python
import math
import jax, jax.numpy as jnp
from jax import lax
import numpy as np

D_MODEL = 1024
BATCH = 8
SEQ = 4096
DEPTH = 4

N_META = 16
SSM_WIDTH = D_MODEL // 2
SSM_GROUP = 16
SSM_GROUPS = SSM_WIDTH // SSM_GROUP
SSM_STATE = 64
GLA_HEADS = 4
GLA_VDIM = D_MODEL // 2
GLA_KDIM = GLA_VDIM // 2
GLA_HK = GLA_KDIM // GLA_HEADS
GLA_HV = GLA_VDIM // GLA_HEADS
GLA_GATE_RANK = 16
GLA_GATE_TAU = 16.0
GLA_CHUNK = 64
D_FF = -(-8 * D_MODEL // (3 * 256)) * 256
EPS = 1e-6
SPLIT_SIZES = (SSM_WIDTH, GLA_KDIM, GLA_KDIM, GLA_VDIM, GLA_VDIM, GLA_GATE_RANK, D_MODEL, D_MODEL)
IN_COLS = SSM_WIDTH + 2 * GLA_KDIM + 2 * GLA_VDIM + GLA_GATE_RANK + 2 * D_MODEL

kernel_name = "hybrid_s5_gla_gated_trunk"


def rmsnorm(x, gain):
    xf = x.astype(jnp.float32)
    y = xf * lax.rsqrt(jnp.mean(xf * xf, axis=-1, keepdims=True) + EPS)
    return (y * gain.astype(jnp.float32)).astype(x.dtype)


def split_cols(z):
    idx = [int(i) for i in np.cumsum(SPLIT_SIZES)[:-1]]
    return jnp.split(z, idx, axis=-1)


def _cmul_scan_op(e1, e2):
    a1r, a1i, b1r, b1i = e1
    a2r, a2i, b2r, b2i = e2
    return (a2r * a1r - a2i * a1i,
            a2r * a1i + a2i * a1r,
            a2r * b1r - a2i * b1i + b2r,
            a2r * b1i + a2i * b1r + b2i)


def s5_branch(u, lam_re, lam_im, log_step, b_re, b_im, c_re, c_im, d_skip, w_glu, b_glu):
    f32 = jnp.float32
    bsz, t, _ = u.shape
    uf = u.astype(f32)
    step = jnp.exp(log_step.astype(f32))[:, None]
    lr = jnp.minimum(lam_re.astype(f32), -1e-4)
    li = lam_im.astype(f32)
    mag = jnp.exp(lr * step)
    ab_re = mag * jnp.cos(li * step)
    ab_im = mag * jnp.sin(li * step)
    den = lr * lr + li * li
    nr = ab_re - 1.0
    coef_re = (nr * lr + ab_im * li) / den
    coef_im = (ab_im * lr - nr * li) / den
    br, bi = b_re.astype(f32), b_im.astype(f32)
    bb_re = coef_re[..., None] * br - coef_im[..., None] * bi
    bb_im = coef_re[..., None] * bi + coef_im[..., None] * br
    ug = uf.reshape(bsz, t, SSM_GROUPS, SSM_GROUP)
    bu_re = jnp.einsum('btgh,gph->btgp', ug, bb_re)
    bu_im = jnp.einsum('btgh,gph->btgp', ug, bb_im)
    a_re = jnp.broadcast_to(ab_re, bu_re.shape)
    a_im = jnp.broadcast_to(ab_im, bu_re.shape)
    _, _, xs_re, xs_im = lax.associative_scan(_cmul_scan_op, (a_re, a_im, bu_re, bu_im), axis=1)
    y = (jnp.einsum('btgp,ghp->btgh', xs_re, c_re.astype(f32))
         - jnp.einsum('btgp,ghp->btgh', xs_im, c_im.astype(f32)))
    y = y.reshape(bsz, t, SSM_WIDTH) + d_skip.astype(f32) * uf
    act = jax.nn.gelu(y)
    out = act * jax.nn.sigmoid(act @ w_glu.astype(f32) + b_glu.astype(f32))
    return out.astype(u.dtype)


def gla_branch(q, k, v, r, a_low, w_alpha, b_alpha, norm_gain):
    f32 = jnp.float32
    bsz, t, _ = q.shape
    log_a = jax.nn.log_sigmoid((a_low @ w_alpha + b_alpha).astype(f32)) / GLA_GATE_TAU
    pad = GLA_CHUNK - N_META

    def chunks(z, hd):
        z = jnp.pad(z.astype(f32), ((0, 0), (pad, 0), (0, 0)))
        n = z.shape[1] // GLA_CHUNK
        return z.reshape(bsz, n, GLA_CHUNK, GLA_HEADS, hd).transpose(0, 3, 1, 2, 4)

    qc = chunks(q, GLA_HK) * (GLA_HK ** -0.5)
    kc = chunks(k, GLA_HK)
    vc = chunks(v, GLA_HV)
    gc = chunks(log_a, GLA_HK)
    bcum = jnp.cumsum(gc, axis=3)
    b_last = bcum[:, :, :, -1:, :]
    q_dec = qc * jnp.exp(bcum)
    k_intra = kc * jnp.exp(-bcum)
    k_state = kc * jnp.exp(b_last - bcum)
    causal = jnp.tril(jnp.ones((GLA_CHUNK, GLA_CHUNK), dtype=bool))
    scores = jnp.where(causal, jnp.einsum('bhncd,bhnsd->bhncs', q_dec, k_intra), 0.0)
    o_intra = jnp.einsum('bhncs,bhnse->bhnce', scores, vc)
    kv = jnp.einsum('bhncd,bhnce->bhnde', k_state, vc)
    decay = jnp.exp(b_last[:, :, :, 0, :])

    def step(state, inp):
        dec, kv_n = inp
        return dec[..., None] * state + kv_n, state

    init = jnp.zeros((bsz, GLA_HEADS, GLA_HK, GLA_HV), f32)
    _, s_prev = lax.scan(step, init, (jnp.moveaxis(decay, 2, 0), jnp.moveaxis(kv, 2, 0)))
    s_prev = jnp.moveaxis(s_prev, 0, 2)
    o = o_intra + jnp.einsum('bhncd,bhnde->bhnce', q_dec, s_prev)
    o = o.transpose(0, 2, 3, 1, 4).reshape(bsz, -1, GLA_HEADS, GLA_HV)[:, pad:]
    o = o * lax.rsqrt(jnp.mean(o * o, axis=-1, keepdims=True) + EPS)
    o = o.reshape(bsz, t, GLA_VDIM) * norm_gain.astype(f32)
    return (jax.nn.silu(r.astype(f32)) * o).astype(q.dtype)


def setup_inputs(seed: int = 0) -> dict:
    key = jax.random.key(seed)
    ks = jax.random.split(key, 28)
    nrm = jax.random.normal
    L, G, P, H = DEPTH, SSM_GROUPS, SSM_STATE, SSM_GROUP
    x = nrm(ks[0], (BATCH, SEQ, D_MODEL), jnp.float32)
    meta = nrm(ks[1], (N_META, D_MODEL), jnp.float32)
    norm1 = 1.0 + 0.02 * nrm(ks[2], (L, D_MODEL), jnp.float32)
    w_in = nrm(ks[3], (L, D_MODEL, IN_COLS), jnp.float32) * D_MODEL ** -0.5
    lam_re = -0.5 + 0.01 * nrm(ks[4], (L, G, P), jnp.float32)
    lam_im = math.pi * jnp.arange(P, dtype=jnp.float32) + 0.01 * nrm(ks[5], (L, G, P), jnp.float32)
    log_step = jax.random.uniform(ks[6], (L, G), jnp.float32, math.log(1e-3), math.log(1e-1))
    b_re = nrm(ks[7], (L, G, P, H), jnp.float32) * (2 * H) ** -0.5
    b_im = nrm(ks[8], (L, G, P, H), jnp.float32) * (2 * H) ** -0.5
    c_re = nrm(ks[9], (L, G, H, P), jnp.float32) * P ** -0.5
    c_im = nrm(ks[10], (L, G, H, P), jnp.float32) * P ** -0.5
    d_skip = nrm(ks[11], (L, SSM_WIDTH), jnp.float32)
    w_glu = nrm(ks[12], (L, SSM_WIDTH, SSM_WIDTH), jnp.float32) * SSM_WIDTH ** -0.5
    b_glu = 0.01 * nrm(ks[13], (L, SSM_WIDTH), jnp.float32)
    w_pa = nrm(ks[14], (L, SSM_WIDTH, D_MODEL), jnp.float32) * SSM_WIDTH ** -0.5
    w_alpha = nrm(ks[15], (L, GLA_GATE_RANK, GLA_KDIM), jnp.float32) * GLA_GATE_RANK ** -0.5
    b_alpha = 0.01 * nrm(ks[16], (L, GLA_KDIM), jnp.float32)
    gla_norm = 1.0 + 0.02 * nrm(ks[17], (L, GLA_VDIM), jnp.float32)
    w_pb = nrm(ks[18], (L, GLA_VDIM, D_MODEL), jnp.float32) * GLA_VDIM ** -0.5
    w_out = nrm(ks[19], (L, D_MODEL, D_MODEL), jnp.float32) * D_MODEL ** -0.5
    norm2 = 1.0 + 0.02 * nrm(ks[20], (L, D_MODEL), jnp.float32)
    w_ff1 = nrm(ks[21], (L, D_MODEL, D_FF), jnp.float32) * D_MODEL ** -0.5
    w_ff3 = nrm(ks[22], (L, D_MODEL, D_FF), jnp.float32) * D_MODEL ** -0.5
    w_ff2 = nrm(ks[23], (L, D_FF, D_MODEL), jnp.float32) * D_FF ** -0.5
    norm_f = 1.0 + 0.02 * nrm(ks[24], (D_MODEL,), jnp.float32)
    return {"x": x, "meta": meta, "norm1": norm1, "w_in": w_in, "lam_re": lam_re, "lam_im": lam_im,
            "log_step": log_step, "b_re": b_re, "b_im": b_im, "c_re": c_re, "c_im": c_im,
            "d_skip": d_skip, "w_glu": w_glu, "b_glu": b_glu, "w_pa": w_pa, "w_alpha": w_alpha,
            "b_alpha": b_alpha, "gla_norm": gla_norm, "w_pb": w_pb, "w_out": w_out, "norm2": norm2,
            "w_ff1": w_ff1, "w_ff3": w_ff3, "w_ff2": w_ff2, "norm_f": norm_f}


def reference(x, meta, norm1, w_in, lam_re, lam_im, log_step, b_re, b_im, c_re, c_im, d_skip,
              w_glu, b_glu, w_pa, w_alpha, b_alpha, gla_norm, w_pb, w_out, norm2,
              w_ff1, w_ff3, w_ff2, norm_f):
    bsz = x.shape[0]
    meta_b = jnp.broadcast_to(meta.astype(x.dtype)[None], (bsz, N_META, D_MODEL))
    h = jnp.concatenate([meta_b, x], axis=1)
    for l in range(DEPTH):
        z = rmsnorm(h, norm1[l])
        u, q, k, v, r, a_low, g_a, g_b = split_cols(z @ w_in[l])
        y_a = s5_branch(u, lam_re[l], lam_im[l], log_step[l], b_re[l], b_im[l], c_re[l], c_im[l],
                        d_skip[l], w_glu[l], b_glu[l]) @ w_pa[l]
        y_b = gla_branch(q, k, v, r, a_low, w_alpha[l], b_alpha[l], gla_norm[l]) @ w_pb[l]
        mixed = jax.nn.sigmoid(g_a) * y_a + jax.nn.sigmoid(g_b) * y_b
        h = h + mixed @ w_out[l]
        z2 = rmsnorm(h, norm2[l])
        h = h + (jax.nn.silu(z2 @ w_ff1[l]) * (z2 @ w_ff3[l])) @ w_ff2[l]
    return rmsnorm(h, norm_f)[:, N_META:]
```

```python
import numpy as np
from contextlib import ExitStack
import concourse.bass as bass
import concourse.mybir as mybir
from concourse.bass_utils import run_bass_kernel_spmd

F32 = mybir.dt.float32
BF16 = mybir.dt.bfloat16
AF = mybir.ActivationFunctionType
ALU = mybir.AluOpType
AP = bass.AP

D = 1024
NM = 16
R = 8
EPS = 1e-6
NCH = 38
CHW = 4096
DERIVED = (8, 9, 10, 11, 12)
HOST_CH = [c for c in range(NCH) if c not in DERIVED]
NS = 6
CH = 8000
NDMASEM = 8
ENGS = ("pe", "dve", "act", "pool", "sp")
SYNC_SAME = {"dve", "act"}


class Prog:
    def __init__(self, nc):
        self.nc = nc
        self.ops = {e: [] for e in ENGS}
        self.nops = {e: 0 for e in ENGS}
        self.signaled = {e: set() for e in ENGS}
        self.sems = {}
        self.seen = {e: {} for e in ENGS}
        self.lastw = {}
        self.readers = {}
        self.dma_n = {e: 0 for e in ENGS}
        self.nbank = 0
        self.held = set()
        self.tag = "init"
        self.tags = {e: [] for e in ENGS}

    def bank(self, hold=False):
        while True:
            b = self.nbank % 8
            self.nbank += 1
            if b not in self.held:
                break
        if hold:
            self.held.add(b)
        return b

    def release(self, b):
        self.held.discard(b)

    def _sem(self, prod, idx):
        k = (prod, idx)
        if k not in self.sems:
            self.sems[k] = self.nc.alloc_semaphore(name=f"s_{prod}_{idx}")
        return self.sems[k]

    def _ref(self, eng, p, t, waits):
        if self.seen[eng].get(p, 0) >= t:
            return
        self.seen[eng][p] = t
        if not p.startswith("dma:"):
            self.signaled[p].add(t)
        waits.append((p, t))

    def _deps(self, eng, reads, writes, force_same=False):
        need = {}

        def add(pt):
            if pt is None:
                return
            p, t = pt
            if need.get(p, 0) < t:
                need[p] = t
        for k in reads:
            add(self.lastw.get(k))
        for k in writes:
            add(self.lastw.get(k))
            for pt in self.readers.get(k, ()):
                add(pt)
        waits = []
        for p, t in need.items():
            if p == eng and not (eng in SYNC_SAME or force_same):
                continue
            self._ref(eng, p, t, waits)
        return waits

    def _commit(self, me, reads, writes):
        for k in reads:
            lst = self.readers.setdefault(k, [])
            lst[:] = [pt for pt in lst if pt[0] != me[0]]
            lst.append(me)
        for k in writes:
            self.lastw[k] = me
            self.readers[k] = []

    def op(self, eng, fn, reads=(), writes=()):
        waits = self._deps(eng, reads, writes)
        self.nops[eng] += 1
        t = self.nops[eng]
        self.ops[eng].append((waits, fn, ("op", t)))
        self.tags[eng].append(self.tag)
        self._commit((eng, t), reads, writes)

    def dma(self, eng, fn, reads=(), writes=()):
        n = self.dma_n[eng]
        self.dma_n[eng] += 1
        slot = n % NDMASEM
        gen = n // NDMASEM + 1
        prod = f"dma:{eng}/{slot}"
        waits = self._deps(eng, reads, writes, force_same=True)
        if gen > 1:
            self._ref(eng, prod, gen - 1, waits)
        self.ops[eng].append((waits, fn, ("dma", self._sem("dma_" + eng, slot))))
        self._commit((prod, gen), reads, writes)

    def barrier(self):
        last = {}
        for e in ENGS:
            if self.nops[e]:
                last[e] = self.nops[e]
            n = self.dma_n[e]
            for slot in range(min(n, NDMASEM)):
                last[f"dma:{e}/{slot}"] = (n - 1 - slot) // NDMASEM + 1
        for e in ENGS:
            waits = []
            for p, t in last.items():
                if p == e:
                    continue
                self._ref(e, p, t, waits)
            if waits:
                self.ops[e].append((waits, None, None))

    def emit(self):
        engmap = {"pe": "tensor", "dve": "vector", "act": "scalar", "pool": "gpsimd", "sp": "sync"}
        ticket = {}
        for e in ENGS:
            for k, idx in enumerate(sorted(self.signaled[e])):
                ticket[(e, idx)] = k + 1

        def semval(p, t):
            if p.startswith("dma:"):
                q, slot = p[4:].split("/")
                return (self._sem("dma_" + q, int(slot)), 16 * t)
            tk = ticket[(p, t)]
            return (self._sem(p, (tk - 1) // CH), (tk - 1) % CH + 1)
        with self.nc.Block() as block:
            for e in ENGS:
                ops = self.ops[e]

                def body(engobj, ops=ops, e=e):
                    for waits, fn, inc in ops:
                        for (p, t) in waits:
                            sv = semval(p, t)
                            engobj.wait_ge(sv[0], sv[1])
                        if fn is None:
                            continue
                        ins = fn(engobj)
                        if inc[0] == "dma":
                            ins.then_inc(inc[1], 16)
                        elif (e, inc[1]) in ticket:
                            tk = ticket[(e, inc[1])]
                            ins.then_inc(self._sem(e, (tk - 1) // CH), 1)
                getattr(block, engmap[e])(body)


def _lhsT_img(W, col0, ncols):
    K = W.shape[0]
    return np.ascontiguousarray(W[:, col0:col0 + ncols].reshape(K // 128, 128, ncols).transpose(1, 0, 2))


def pack_layer_weights(inp, l):
    out = np.zeros((len(HOST_CH), 128, CHW), np.float32)
    w_in = inp["w_in"][l]

    def put(c, img, off=0):
        flat = img.reshape(128, -1)
        out[HOST_CH.index(c), :, off:off + flat.shape[1]] = flat
    put(0, _lhsT_img(w_in, 0, 512))
    put(1, _lhsT_img(w_in, 512, 512))
    put(2, _lhsT_img(w_in, 1536, 512))
    put(3, _lhsT_img(w_in, 768, 256))
    put(3, _lhsT_img(w_in, 2048, 16), 2048)
    put(4, _lhsT_img(w_in, 1024, 512))
    put(5, _lhsT_img(inp["w_pb"][l], 0, 1024))
    put(6, _lhsT_img(w_in, 3088, 512))
    put(7, _lhsT_img(w_in, 3600, 512))
    put(13, _lhsT_img(inp["w_glu"][l], 0, 512))
    put(14, _lhsT_img(inp["w_pa"][l], 0, 1024))
    put(15, _lhsT_img(w_in, 2064, 512))
    put(16, _lhsT_img(w_in, 2576, 512))
    put(17, _lhsT_img(inp["w_out"][l], 0, 512))
    put(18, _lhsT_img(inp["w_out"][l], 512, 512))
    w1, w3, w2 = inp["w_ff1"][l], inp["w_ff3"][l], inp["w_ff2"][l]
    for i in range(11):
        img = np.stack([_lhsT_img(w1, 256 * i, 256), _lhsT_img(w3, 256 * i, 256)], axis=1)
        put(19 + i, img)
    for m in range(8):
        put(30 + m, _lhsT_img(w2, 128 * m, 128))
    return out


def pack_s5(inp, l):
    lam = np.zeros((128, 3, 16), np.float32)
    bS = np.zeros((128, 2, 16, 32), np.float32)
    cS = np.zeros((128, 2, 16, 32), np.float32)
    for q in range(16):
        for gg in range(2):
            g = 2 * q + gg
            sl = slice(gg * 64, gg * 64 + 64)
            lam[sl, 0, q] = inp["lam_re"][l, g]
            lam[sl, 1, q] = inp["lam_im"][l, g]
            lam[sl, 2, q] = inp["log_step"][l, g]
            bS[sl, 0, q, gg * 16:gg * 16 + 16] = inp["b_re"][l, g]
            bS[sl, 1, q, gg * 16:gg * 16 + 16] = inp["b_im"][l, g]
            cS[sl, 0, q, gg * 16:gg * 16 + 16] = inp["c_re"][l, g].T
            cS[sl, 1, q, gg * 16:gg * 16 + 16] = inp["c_im"][l, g].T
    return lam, bS, cS


def pack_small(inp, L):
    sm = np.zeros((128, L, 28), np.float32)
    for l in range(L):
        sm[:, l, 0:8] = inp["norm1"][l].reshape(8, 128).T
        sm[:, l, 8:16] = inp["norm2"][l].reshape(8, 128).T
        sm[:, l, 16:20] = inp["d_skip"][l].reshape(4, 128).T
        sm[:, l, 20:24] = inp["b_glu"][l].reshape(4, 128).T
        sm[:, l, 24:28] = inp["gla_norm"][l].reshape(4, 128).T
    gf = np.ascontiguousarray(inp["norm_f"].reshape(8, 128).T)
    walp = np.zeros((32, L, 256), np.float32)
    for l in range(L):
        walp[0:16, l] = inp["w_alpha"][l]
        walp[16, l] = inp["b_alpha"][l]
    return sm, gf, walp


DBG = {"on": False, "names": []}


def build_nc(L, tiles):
    T = tiles[-1][0] + tiles[-1][1]
    TOUT = T - NM
    nc = bass.Bass("TRN2", target_bir_lowering=False)
    P = Prog(nc)
    din = lambda n, s: nc.dram_tensor(n, s, F32, kind="ExternalInput").ap()
    xT = din("xT", [D, T])
    wpack = din("wpack", [L, len(HOST_CH), 128, CHW])
    s5lam = din("s5lam", [L, 128, 3, 16])
    s5b = din("s5b", [L, 128, 2, 16, 32])
    s5c = din("s5c", [L, 128, 2, 16, 32])
    small = din("small", [128, L, 28])
    gfin = din("gfin", [128, 8])
    walp_d = din("walp", [32, L, 256])
    outT = nc.dram_tensor("outT", [D, TOUT], F32, kind="ExternalOutput").ap()
    wbf = nc.dram_tensor("wbf", [L, NCH, 128, CHW], BF16, kind="Internal").ap()

    sb = lambda n, s, d=F32: nc.alloc_sbuf_tensor(n, s, d)
    ps = nc.alloc_psum_tensor("ps", [128, 8, 512], F32)
    PS = lambda b: ("ps", b)

    wring = sb("wring", [128, NS, CHW], BF16)
    ident = sb("ident", [128, 128])
    onesD = sb("onesD", [128, 128], BF16)
    onesH = sb("onesH", [128, 128], BF16)
    maskU = sb("maskU", [64, 64])
    maskL = sb("maskL", [64, 64])
    causal = sb("causal", [64, 64])
    cst = sb("cst", [128, 4])
    smalls = sb("smalls", [128, L, 28])
    gf = sb("gf", [128, 8])
    walp = sb("walp_s", [32, L, 256])
    A1 = sb("A1", [128, L, 2, 16])
    A2 = sb("A2", [128, L, 2, 16])
    Gcar = sb("Gcar", [128, L, 2, 16])
    Scar = sb("Scar", [64, L, 4, 128])
    tmpc = sb("tmpc", [128, 128])

    def v(e_, f):
        return f

    P.op("pool", lambda e: e.memset(tmpc[:], 1.0), writes=["tmpc"])
    P.op("pool", lambda e: e.affine_select(out=ident[:], in_=tmpc[:], pattern=[[1, 128]], compare_op=ALU.is_equal,
                                           fill=0.0, base=0, channel_multiplier=-1), reads=["tmpc"], writes=["ident"])
    P.op("pool", lambda e: e.memset(onesD[:], 1.0 / 1024), writes=["onesD"])
    P.op("pool", lambda e: e.memset(onesH[:], 1.0 / 128), writes=["onesH"])
    P.op("pool", lambda e: e.memset(cst[:, 0:1], EPS), writes=["cst"])
    P.op("pool", lambda e: e.memset(cst[:, 1:2], 1.0), writes=["cst"])
    P.op("pool", lambda e: e.memset(cst[:, 2:3], 0.0), writes=["cst"])
    P.op("pool", lambda e: e.affine_select(out=causal[:], in_=tmpc[0:64, 0:64], pattern=[[1, 64]], compare_op=ALU.is_ge,
                                           fill=0.0, base=0, channel_multiplier=-1), reads=["tmpc"], writes=["causal"])
    P.op("pool", lambda e: e.tensor_scalar(out=maskU[:], in0=causal[:], scalar1=-1.0 / 16, scalar2=None, op0=ALU.mult),
         reads=["causal"], writes=["maskU"])
    P.op("pool", lambda e: e.tensor_scalar(out=maskL[:], in0=causal[:], scalar1=1.0 / 16, scalar2=-1.0 / 16, op0=ALU.mult,
                                           op1=ALU.add), reads=["causal"], writes=["maskL"])
    P.dma("sp", lambda e: e.dma_start(out=smalls[:], in_=small[:, :, :]), writes=["smalls"])
    P.dma("sp", lambda e: e.dma_start(out=gf[:], in_=gfin[:, :]), writes=["gf"])
    P.dma("sp", lambda e: e.dma_start(out=walp[:], in_=walp_d[:, :, :]), writes=["walp"])
    P.op("pool", lambda e: e.memset(Gcar[:], 0.0), writes=["Gcar"])
    P.op("pool", lambda e: e.memset(Scar[:], 0.0), writes=["Scar"])

    n = 0
    for l in range(L):
        for ci, c in enumerate(HOST_CH):
            s = n % NS
            n += 1
            P.dma("pool", lambda e, l=l, ci=ci, s=s: e.dma_start(out=wring[:, s, :], in_=wpack[l, ci]),
                  writes=[("w", s)])
            P.dma("sp", lambda e, l=l, c=c, s=s: e.dma_start(out=wbf[l, c], in_=wring[:, s, :]),
                  reads=[("w", s)], writes=[("wbf", l, c)])

    with ExitStack() as es:
        tb = lambda nme, s, d=F32: es.enter_context(nc.sbuf_tensor(nme, s, d))
        lam = tb("p_lam", [128, 3, 16])
        bS = tb("p_bS", [128, 2, 16, 32])
        cS = tb("p_cS", [128, 2, 16, 32])
        ncSi = tb("p_ncSi", [128, 16, 32])
        sc = tb("p_sc", [128, 32, 16])
        Mr = [tb(f"p_Mr{i}", [128, 16, 32]) for i in range(2)]
        Mi = [tb(f"p_Mi{i}", [128, 16, 32]) for i in range(2)]
        t32 = [tb(f"p_t32{i}", [128, 16, 32]) for i in range(2)]
        BD = tb("p_BD", [128, 128])
        stg = {c: tb(f"p_stg{c}", [128, CHW], BF16) for c in DERIVED}
        P.op("dve", lambda e: e.memset(BD[:], 0.0), writes=["BD"])
        for r in range(4):
            P.op("dve", lambda e, r=r: e.memset(BD[32 * r:32 * r + 32, 32 * r:32 * r + 32], 1.0), reads=["BD"], writes=["BD"])
        WBv = {0: stg[8][:].rearrange("p (c i s) -> p c i s", c=4, i=8), 1: stg[9][:].rearrange("p (c i s) -> p c i s", c=4, i=8)}
        KTv = stg[10][:].rearrange("p (c i s) -> p c i s", c=4, i=8)
        CAv = {0: stg[11][:].rearrange("p (q j s) -> p q j s", q=16, j=8), 1: stg[12][:].rearrange("p (q j s) -> p q j s", q=16, j=8)}
        S = lambda i: sc[:, i, :]
        K = lambda i: ("sc", i)

        def bc(a16):
            return AP(a16.tensor, a16.offset, [list(a16.ap[0]), list(a16.ap[1]), [0, 32]])

        def tt(out, a, b, op, r, w, eng="dve"):
            P.op(eng, lambda e: e.tensor_tensor(out=out, in0=a, in1=b, op=op), reads=r, writes=w)

        def ts(out, a, s1, s2, op0, op1, r, w, eng="dve"):
            if op1 is None:
                P.op(eng, lambda e: e.tensor_scalar(out=out, in0=a, scalar1=s1, scalar2=None, op0=op0), reads=r, writes=w)
            else:
                P.op(eng, lambda e: e.tensor_scalar(out=out, in0=a, scalar1=s1, scalar2=s2, op0=op0, op1=op1), reads=r, writes=w)

        def act(out, a, fn, r, w, **kw):
            P.op("act", lambda e: e.activation(out=out, in_=a, func=fn, **kw), reads=r, writes=w)

        TWO_PI = 2.0 * np.pi
        C1 = 6.28125
        C2 = TWO_PI - C1
        MAGIC = 12582912.0

        def sin_of(dst, src_i, shift, ktmp):
            y, t1, t2 = ktmp
            ts(S(y), S(src_i), shift, None, ALU.add, None, [K(src_i)], [K(y)])
            ts(S(t1), S(y), 1.0 / TWO_PI, None, ALU.mult, None, [K(y)], [K(t1)])
            ts(S(t2), S(t1), MAGIC, None, ALU.add, None, [K(t1)], [K(t2)])
            ts(S(t1), S(t2), -MAGIC, None, ALU.add, None, [K(t2)], [K(t1)])
            P.op("dve", lambda e: e.scalar_tensor_tensor(out=S(t2), in0=S(t1), scalar=-C1, in1=S(y), op0=ALU.mult, op1=ALU.add),
                 reads=[K(t1), K(y)], writes=[K(t2)])
            P.op("dve", lambda e: e.scalar_tensor_tensor(out=S(y), in0=S(t1), scalar=-C2, in1=S(t2), op0=ALU.mult, op1=ALU.add),
                 reads=[K(t1), K(t2)], writes=[K(y)])
            ts(S(y), S(y), float(np.pi), float(-np.pi), ALU.min, ALU.max, [K(y)], [K(y)])
            act(S(dst), S(y), AF.Sin, [K(y)], [K(dst)])

        for l in range(L):
            P.dma("sp", lambda e, l=l: e.dma_start(out=lam[:], in_=s5lam[l]), writes=["lam"])
            P.dma("sp", lambda e, l=l: e.dma_start(out=bS[:], in_=s5b[l]), writes=["bS"])
            P.dma("sp", lambda e, l=l: e.dma_start(out=cS[:], in_=s5c[l]), writes=["cS"])
            ts(ncSi[:], cS[:, 1], -1.0, None, ALU.mult, None, ["cS"], ["ncSi"])
            STEP, LR, LI, MAG, ANG, COS, SIN, ARE, AIM, DEN, NR, CFR, CFI, T0, T1, T2, T3, PR, PI_, PR2, PI2 = range(21)
            act(S(STEP), lam[:, 2, :], AF.Exp, ["lam"], [K(STEP)])
            ts(S(LR), lam[:, 0, :], -1e-4, None, ALU.min, None, ["lam"], [K(LR)])
            ts(S(LI), lam[:, 1, :], 1.0, None, ALU.mult, None, ["lam"], [K(LI)])
            tt(S(T0), S(LR), S(STEP), ALU.mult, [K(LR), K(STEP)], [K(T0)])
            act(S(MAG), S(T0), AF.Exp, [K(T0)], [K(MAG)])
            tt(S(ANG), S(LI), S(STEP), ALU.mult, [K(LI), K(STEP)], [K(ANG)])
            sin_of(SIN, ANG, 0.0, (T1, T2, T3))
            sin_of(COS, ANG, float(np.pi / 2), (T1, T2, T3))
            tt(S(ARE), S(MAG), S(COS), ALU.mult, [K(MAG), K(COS)], [K(ARE)])
            tt(S(AIM), S(MAG), S(SIN), ALU.mult, [K(MAG), K(SIN)], [K(AIM)])
            tt(S(T0), S(LR), S(LR), ALU.mult, [K(LR)], [K(T0)])
            tt(S(T1), S(LI), S(LI), ALU.mult, [K(LI)], [K(T1)])
            tt(S(DEN), S(T0), S(T1), ALU.add, [K(T0), K(T1)], [K(DEN)])
            P.op("dve", lambda e: e.reciprocal(S(DEN), S(DEN)), reads=[K(DEN)], writes=[K(DEN)])
            ts(S(NR), S(ARE), -1.0, None, ALU.add, None, [K(ARE)], [K(NR)])
            tt(S(T0), S(NR), S(LR), ALU.mult, [K(NR), K(LR)], [K(T0)])
            tt(S(T1), S(AIM), S(LI), ALU.mult, [K(AIM), K(LI)], [K(T1)])
            tt(S(T0), S(T0), S(T1), ALU.add, [K(T0), K(T1)], [K(T0)])
            tt(S(CFR), S(T0), S(DEN), ALU.mult, [K(T0), K(DEN)], [K(CFR)])
            tt(S(T0), S(AIM), S(LR), ALU.mult, [K(AIM), K(LR)], [K(T0)])
            tt(S(T1), S(NR), S(LI), ALU.mult, [K(NR), K(LI)], [K(T1)])
            tt(S(T0), S(T0), S(T1), ALU.subtract, [K(T0), K(T1)], [K(T0)])
            tt(S(CFI), S(T0), S(DEN), ALU.mult, [K(T0), K(DEN)], [K(CFI)])

            def cmul(dr, di, ar_i, ai_i, xr, xi, rk, wk):
                tt(t32[0][:], xr, bc(S(ar_i)), ALU.mult, rk + [K(ar_i)], ["t32a"])
                tt(t32[1][:], xi, bc(S(ai_i)), ALU.mult, rk + [K(ai_i)], ["t32b"])
                tt(dr, t32[0][:], t32[1][:], ALU.subtract, ["t32a", "t32b"], [wk[0]])
                tt(t32[0][:], xi, bc(S(ar_i)), ALU.mult, rk + [K(ar_i)], ["t32a"])
                tt(t32[1][:], xr, bc(S(ai_i)), ALU.mult, rk + [K(ai_i)], ["t32b"])
                tt(di, t32[0][:], t32[1][:], ALU.add, ["t32a", "t32b"], [wk[1]])

            cur = 0
            cmul(Mr[0][:], Mi[0][:], CFR, CFI, bS[:, 0], bS[:, 1], ["bS"], [("M", 0, 0), ("M", 0, 1)])
            for tau in range(R):
                mk = [("M", cur, 0), ("M", cur, 1)]
                for c in range(4):
                    b = P.bank()
                    mr = Mr[cur][:, 4 * c:4 * c + 4, :].rearrange("p a b -> p (a b)")
                    mi = Mi[cur][:, 4 * c:4 * c + 4, :].rearrange("p a b -> p (a b)")
                    cr = cS[:, 0, 4 * c:4 * c + 4, :].rearrange("p a b -> p (a b)")
                    nci = ncSi[:, 4 * c:4 * c + 4, :].rearrange("p a b -> p (a b)")
                    P.op("pe", lambda e, b=b, mr=mr, cr=cr: e.matmul(ps[:, b, 0:128], mr, cr, start=True, stop=False),
                         reads=[mk[0], "cS"], writes=[PS(b)])
                    P.op("pe", lambda e, b=b, mi=mi, nci=nci: e.matmul(ps[:, b, 0:128], mi, nci, start=False, stop=True),
                         reads=[mk[1], "ncSi"], writes=[PS(b)])
                    P.op("dve", lambda e, b=b, c=c, tau=tau: e.tensor_tensor(out=KTv[:, c, tau, :], in0=ps[:, b, 0:128], in1=BD[:],
                                                                               op=ALU.mult), reads=[PS(b), "BD"], writes=[("stg", 10)])
                    for part, Mx in ((0, mr), (1, mi)):
                        b2 = P.bank()
                        P.op("pe", lambda e, b2=b2, Mx=Mx: e.transpose(ps[:, b2, 0:128], Mx, ident[:]),
                             reads=[mk[part], "ident"], writes=[PS(b2)])
                        P.op("act", lambda e, b2=b2, part=part, c=c, tau=tau: e.copy(WBv[part][:, c, R - 1 - tau, :], ps[:, b2, 0:128]),
                             reads=[PS(b2)], writes=[("stg", 8 + part)])
                if tau < R - 1:
                    nxt = 1 - cur
                    cmul(Mr[nxt][:], Mi[nxt][:], ARE, AIM, Mr[cur][:], Mi[cur][:], mk, [("M", nxt, 0), ("M", nxt, 1)])
                    cur = nxt
            ts(S(PR), S(ARE), 1.0, None, ALU.mult, None, [K(ARE)], [K(PR)])
            ts(S(PI_), S(AIM), 1.0, None, ALU.mult, None, [K(AIM)], [K(PI_)])
            pr, pi, pr2, pi2 = PR, PI_, PR2, PI2
            for j in range(R):
                tt(t32[0][:], cS[:, 0], bc(S(pr)), ALU.mult, ["cS", K(pr)], ["t32a"])
                tt(t32[1][:], cS[:, 1], bc(S(pi)), ALU.mult, ["cS", K(pi)], ["t32b"])
                tt(CAv[0][:, :, j, :], t32[0][:], t32[1][:], ALU.subtract, ["t32a", "t32b"], [("stg", 11)])
                tt(t32[0][:], ncSi[:], bc(S(pr)), ALU.mult, ["ncSi", K(pr)], ["t32a"])
                tt(t32[1][:], cS[:, 0], bc(S(pi)), ALU.mult, ["cS", K(pi)], ["t32b"])
                tt(CAv[1][:, :, j, :], t32[0][:], t32[1][:], ALU.subtract, ["t32a", "t32b"], [("stg", 12)])
                if j < R - 1:
                    tt(S(T0), S(pr), S(ARE), ALU.mult, [K(pr), K(ARE)], [K(T0)])
                    tt(S(T1), S(pi), S(AIM), ALU.mult, [K(pi), K(AIM)], [K(T1)])
                    tt(S(pr2), S(T0), S(T1), ALU.subtract, [K(T0), K(T1)], [K(pr2)])
                    tt(S(T0), S(pr), S(AIM), ALU.mult, [K(pr), K(AIM)], [K(T0)])
                    tt(S(T1), S(pi), S(ARE), ALU.mult, [K(pi), K(ARE)], [K(T1)])
                    tt(S(pi2), S(T0), S(T1), ALU.add, [K(T0), K(T1)], [K(pi2)])
                    pr, pi, pr2, pi2 = pr2, pi2, pr, pi
            for h2 in range(2):
                ts(A1[:, l, h2, :], S(pr), 1.0, None, ALU.mult, None, [K(pr)], ["A1"])
            ts(A2[:, l, 0, :], S(pi), -1.0, None, ALU.mult, None, [K(pi)], ["A2"])
            ts(A2[:, l, 1, :], S(pi), 1.0, None, ALU.mult, None, [K(pi)], ["A2"])
            for c in DERIVED:
                P.dma("sp", lambda e, l=l, c=c: e.dma_start(out=wbf[l, c], in_=stg[c][:]), reads=[("stg", c)],
                      writes=[("wbf", l, c)])
        P.barrier()

    NT = 512
    hT = sb("hT", [128, 8, NT])
    zT = sb("zT", [128, 8, NT], BF16)
    hid = sb("hid", [128, 22, NT], BF16)
    rstd = sb("rstd", [128, NT])
    uT = sb("uT", [128, 4, NT], BF16)
    qk = sb("qk", [64, 8, NT], BF16)
    srT = sb("srT", [128, 4, NT], BF16)
    alT = sb("alT", [32, NT])
    mixT = sb("mixT", [128, 8, NT], BF16)
    ftmp = [sb(f"ftmp{i}", [128, NT]) for i in range(2)]
    vtok = sb("vtok", [64, 8, 512], BF16)
    kst = sb("kst", [64, 8, 256], BF16)
    SPt = [sb(f"SPt{i}", [64, 256]) for i in range(2)]
    E3t = [sb(f"E3t{i}", [64, 256]) for i in range(2)]
    etm = [sb(f"etm{i}", [64, 256]) for i in range(2)]
    E12 = [sb(f"E12_{i}", [64, 2, 4, 64]) for i in range(2)]
    dec = sb("dec", [64, 8, 4])
    scm = [sb(f"scm{i}", [64, 8, 64], BF16) for i in range(4)]
    Sall = [sb(f"Sall{i}", [64, 9, 128]) for i in range(2)]
    Sbf = [sb(f"Sbf{i}", [64, 8, 128], BF16) for i in range(4)]
    onT = sb("onT", [128, 4, NT], BF16)
    Gs = sb("Gs", [128, 65, 2, 16])
    Gbf = hid[:, 17:21, :].rearrange("p m n -> p (m n)").rearrange("p (a q b) -> p a q b", a=2, q=16)
    GBK = [("hid", 17 + i) for i in range(4)]
    gt = [sb(f"gt{i}", [128, 2, 16]) for i in range(2)]
    yT = sb("yT", [128, 4, NT])

    P.op("pool", lambda e: e.memset(alT[:], 1.0), writes=["alT"])

    ORDER = [0, 8, 9] + [c for c in range(1, NCH) if c not in (8, 9)]
    sched = [(ti, l, c) for ti in range(len(tiles)) for l in range(L) for c in ORDER]
    wstate = {"issued": 0, "next": 0}

    def issue_loads(upto):
        while wstate["issued"] < min(upto, len(sched)):
            i = wstate["issued"]
            _, l, c = sched[i]
            s = i % NS
            P.dma("sp", lambda e, l=l, c=c, s=s: e.dma_start(out=wring[:, s, :], in_=wbf[l, c]),
                  reads=[("wbf", l, c)], writes=[("w", s)])
            wstate["issued"] += 1

    def W(ti, l, c, keep=0):
        i = wstate["next"]
        assert sched[i] == (ti, l, c), (sched[i], (ti, l, c))
        wstate["next"] += 1
        issue_loads(max(i + 1, i - keep + NS))
        s = i % NS
        return wring[:, s, :], ("w", s)

    def dump(name, tens, shape, dtype, keys, ti, l):
        if not DBG["on"] or not (ti == 1 and l == 0):
            return
        dd = nc.dram_tensor("dbg_" + name, list(shape), dtype, kind="ExternalOutput").ap()
        DBG["names"].append("dbg_" + name)
        P.dma("sp", lambda e: e.dma_start(out=dd, in_=tens), reads=keys, writes=["dbg_" + name])

    def mm(out, lhsT, rhs, start, stop, reads, writes, **kw):
        P.op("pe", lambda e: e.matmul(out, lhsT, rhs, start=start, stop=stop, **kw), reads=reads, writes=writes)

    def rmsnorm(N, gain_col, dst, dkey, final_store=None):
        for dt in range(8):
            P.op("act", lambda e, dt=dt: e.activation(out=hid[:, dt, :N], in_=hT[:, dt, :N], func=AF.Square),
                 reads=[("hT", dt)], writes=[("hid", dt)])
        b = P.bank()
        for dt in range(8):
            mm(ps[:, b, :N], onesD[:], hid[:, dt, :N], dt == 0, dt == 7, [("hid", dt), "onesD"], [PS(b)])
        P.op("act", lambda e: e.activation(out=rstd[:, :N], in_=ps[:, b, :N], func=AF.Ln, bias=cst[:, 0:1]),
             reads=[PS(b), "cst"], writes=["rstd"])
        P.op("act", lambda e: e.activation(out=rstd[:, :N], in_=rstd[:, :N], func=AF.Exp, scale=-0.5),
             reads=["rstd"], writes=["rstd"])
        if final_store is not None:
            for dt in range(8):
                x = dt % 2
                P.op("dve", lambda e, dt=dt, x=x: e.scalar_tensor_tensor(out=ftmp[x][:, :N], in0=hT[:, dt, :N], scalar=gain_col(dt),
                                                                         in1=rstd[:, :N], op0=ALU.mult, op1=ALU.mult),
                     reads=[("hT", dt), "rstd", "smalls", "gf"], writes=[f"ftmp{x}"])
                P.dma("sp", lambda e, dt=dt, x=x: e.dma_start(out=outT[128 * dt:128 * dt + 128, final_store:final_store + N],
                                                              in_=ftmp[x][:, :N]), reads=[f"ftmp{x}"], writes=[("outT", dt)])
            return
        for dt in range(8):
            P.op("dve", lambda e, dt=dt: e.scalar_tensor_tensor(out=dst[:, dt, :N], in0=hT[:, dt, :N], scalar=gain_col(dt),
                                                                 in1=rstd[:, :N], op0=ALU.mult, op1=ALU.mult),
                 reads=[("hT", dt), "rstd", "smalls", "gf"], writes=[(dkey, dt)])

    zkeys = [("zT", kt) for kt in range(8)]

    def tile_layer(ti, l):
        t0, N = tiles[ti]
        nb = N // R
        if N >= 64:
            chunks = [(64 * i, 64) for i in range(N // 64)]
        else:
            chunks = [(0, N)]
        nch = len(chunks)
        sm = lambda a, b_: smalls[:, l, a:b_]

        if l == 0:
            P.dma("sp", lambda e: e.dma_start(out=hT[:, :, :N], in_=xT.rearrange("(dt p) t -> p dt t", p=128)[:, :, t0:t0 + N]),
                  writes=[("hT", dt) for dt in range(8)])
        P.tag = "norm1"
        rmsnorm(N, lambda dt: smalls[:, l, dt:dt + 1], zT, "zT")

        dump("zT", zT[:], [128, 8, 512], BF16, zkeys, ti, l)
        P.tag = "inproj_u"
        w, wk = W(ti, l, 0)
        wv = w.rearrange("p (k c) -> p k c", k=8)
        for m in range(4):
            b = P.bank()
            for kt in range(8):
                mm(ps[:, b, :N], wv[:, kt, 128 * m:128 * m + 128], zT[:, kt, :N], kt == 0, kt == 7, [wk, zkeys[kt]], [PS(b)])
            P.op("act", lambda e, b=b, m=m: e.copy(uT[:, m, :N], ps[:, b, :N]), reads=[PS(b)], writes=[("uT", m)])
        P.tag = "s5_A"
        WB = [W(ti, l, 8), W(ti, l, 9, keep=1)]
        for c in range(4):
            for part in range(2):
                wbv = WB[part][0].rearrange("p (c i s) -> p c i s", c=4, i=8)
                bks = [P.bank() for _ in range(4)]
                for i in range(R):
                    for r in range(4):
                        kw = {"tile_position": (96, 0)} if r == 3 else {}
                        rhs = uT[32 * r:32 * r + 32, c, :N].rearrange("p (n s) -> p s n", s=R)[:, i, :]
                        mm(ps[:, bks[r], :nb], wbv[32 * r:32 * r + 32, c, i, :], rhs, i == 0, i == R - 1, [WB[part][1], ("uT", c)],
                           [PS(bks[r])], **kw)
                for r in range(4):
                    P.op("act", lambda e, be=bks[r], part=part, q=4 * c + r: e.copy(Gs[:, 1:nb + 1, part, q], ps[:, be, :nb]),
                         reads=[PS(bks[r])], writes=["Gs"])
        P.tag = "s5_scan"
        P.op("pool", lambda e: e.tensor_copy(Gs[:, 0, :, :], Gcar[:, l, :, :]), reads=["Gcar"], writes=["Gs"])
        for b_ in range(nb):
            prev = Gs[:, b_, :, :]
            prevsw = AP(Gs, (b_ * 32 + 16), [[65 * 32, 128], [-16, 2], [1, 16]])
            P.op("pool", lambda e, prev=prev: e.tensor_tensor(out=gt[0][:], in0=prev, in1=A1[:, l, :, :], op=ALU.mult),
                 reads=["Gs", "A1"], writes=["gt0"])
            P.op("pool", lambda e, prevsw=prevsw: e.tensor_tensor(out=gt[1][:], in0=prevsw, in1=A2[:, l, :, :], op=ALU.mult),
                 reads=["Gs", "A2"], writes=["gt1"])
            P.op("pool", lambda e: e.tensor_tensor(out=gt[0][:], in0=gt[0][:], in1=gt[1][:], op=ALU.add), reads=["gt0", "gt1"], writes=["gt0"])
            P.op("pool", lambda e, b_=b_: e.tensor_tensor(out=Gs[:, b_ + 1, :, :], in0=Gs[:, b_ + 1, :, :], in1=gt[0][:], op=ALU.add),
                 reads=["Gs", "gt0"], writes=["Gs"])
        P.op("pool", lambda e: e.tensor_copy(Gcar[:, l, :, :], Gs[:, nb, :, :]), reads=["Gs"], writes=["Gcar"])
        P.op("pool", lambda e: e.tensor_copy(Gbf[:, :, :, :nb].rearrange("p a q b -> p b (a q)"),
                                             Gs[:, 0:nb, :, :].rearrange("p b a q -> p b (a q)")), reads=["Gs"], writes=GBK)
        P.tag = "inproj_qk"
        w, wk = W(ti, l, 1)
        wv = w.rearrange("p (k c) -> p k c", k=8)
        for m in range(8):
            b = P.bank()
            for kt in range(8):
                mm(ps[:64, b, :N], wv[:, kt, 64 * m:64 * m + 64], zT[:, kt, :N], kt == 0, kt == 7, [wk, zkeys[kt]], [PS(b)])
            P.op("dve", lambda e, b=b, m=m: e.tensor_copy(qk[:, m, :N], ps[:64, b, :N]), reads=[PS(b)], writes=[("qk", m)])
        P.tag = "inproj_r"
        w, wk = W(ti, l, 2)
        wv = w.rearrange("p (k c) -> p k c", k=8)
        for m in range(4):
            b = P.bank()
            for kt in range(8):
                mm(ps[:, b, :N], wv[:, kt, 128 * m:128 * m + 128], zT[:, kt, :N], kt == 0, kt == 7, [wk, zkeys[kt]], [PS(b)])
            P.op("act", lambda e, b=b, m=m: e.activation(out=srT[:, m, :N], in_=ps[:, b, :N], func=AF.Silu),
                 reads=[PS(b)], writes=[("srT", m)])
        P.tag = "gla_gates"
        w, wk = W(ti, l, 3)
        wkt = w[:, 0:2048].rearrange("p (k c) -> p k c", k=8)
        wal = w[:, 2048:2176].rearrange("p (k c) -> p k c", k=8)
        b = P.bank()
        for kt in range(8):
            mm(ps[:16, b, :N], wal[:, kt, :], zT[:, kt, :N], kt == 0, kt == 7, [wk, zkeys[kt]], [PS(b)])
        P.op("dve", lambda e, b=b: e.tensor_copy(alT[0:16, :N], ps[:16, b, :N]), reads=[PS(b)], writes=["alT"])
        for ci, (c0, Lc) in enumerate(chunks):
            x = ci % 2
            bl = P.bank()
            mm(ps[:Lc, bl, :256], alT[0:17, c0:c0 + Lc], walp[0:17, l, :], True, True, ["alT", "walp"], [PS(bl)])
            P.op("act", lambda e, bl=bl, x=x, Lc=Lc: e.activation(out=etm[x][:Lc, :], in_=ps[:Lc, bl, :256], func=AF.Exp, scale=-1.0),
                 reads=[PS(bl)], writes=[("etm", x)])
            P.op("act", lambda e, x=x, Lc=Lc: e.activation(out=SPt[x][:Lc, :], in_=etm[x][:Lc, :], func=AF.Ln, bias=cst[:Lc, 1:2]),
                 reads=[("etm", x), "cst"], writes=[("SPt", x)])
            bb = P.bank()
            for h in range(4):
                mm(ps[:64, bb, 64 * h:64 * h + Lc], SPt[x][:Lc, 64 * h:64 * h + 64], maskU[:Lc, :Lc], True, True,
                   [("SPt", x), "maskU"], [PS(bb)])
            pv = ps[:64, bb, 0:256].rearrange("p (h t) -> p h t", h=4)[:, :, :Lc]
            P.op("act", lambda e, x=x, pv=pv, Lc=Lc: e.activation(out=E12[x][:, 0, :, :Lc], in_=pv, func=AF.Exp),
                 reads=[PS(bb)], writes=[("E12", x, 0)])
            P.op("act", lambda e, x=x, pv=pv, Lc=Lc: e.activation(out=E12[x][:, 1, :, :Lc], in_=pv, func=AF.Exp, scale=-1.0),
                 reads=[PS(bb)], writes=[("E12", x, 1)])
            P.op("dve", lambda e, x=x, c0=c0, Lc=Lc: e.scalar_tensor_tensor(out=qk[:, 0:4, c0:c0 + Lc], in0=qk[:, 0:4, c0:c0 + Lc],
                                                                            scalar=0.125, in1=E12[x][:, 0, :, :Lc], op0=ALU.mult,
                                                                            op1=ALU.mult),
                 reads=[("E12", x, 0)] + [("qk", m) for m in range(4)], writes=[("qk", m) for m in range(4)])
            P.op("dve", lambda e, x=x, c0=c0, Lc=Lc: e.tensor_tensor(out=qk[:, 4:8, c0:c0 + Lc], in0=qk[:, 4:8, c0:c0 + Lc],
                                                                     in1=E12[x][:, 1, :, :Lc], op=ALU.mult),
                 reads=[("E12", x, 1)] + [("qk", m) for m in range(4, 8)], writes=[("qk", m) for m in range(4, 8)])
            P.op("dve", lambda e, x=x, ci=ci, Lc=Lc: e.tensor_copy(dec[:, ci, :], E12[x][:, 0, :, Lc - 1]),
                 reads=[("E12", x, 0)], writes=[("dec", ci)])
            br = P.bank()
            mm(ps[:Lc, br, :256], maskL[:Lc, :Lc], SPt[x][:Lc, :], True, True, [("SPt", x), "maskL"], [PS(br)])
            P.op("act", lambda e, br=br, x=x, Lc=Lc: e.activation(out=E3t[x][:Lc, :], in_=ps[:Lc, br, :256], func=AF.Exp),
                 reads=[PS(br)], writes=[("E3t", x)])
            bk = P.bank()
            for kt in range(8):
                mm(ps[:Lc, bk, :256], zT[:, kt, c0:c0 + Lc], wkt[:, kt, :], kt == 0, kt == 7, [wk, zkeys[kt]], [PS(bk)])
            P.op("dve", lambda e, bk=bk, x=x, ci=ci, Lc=Lc: e.tensor_tensor(out=kst[:Lc, ci, :], in0=ps[:Lc, bk, :256], in1=E3t[x][:Lc, :],
                                                                            op=ALU.mult),
                 reads=[PS(bk), ("E3t", x)], writes=[("kst", ci)])
        P.tag = "v_tok"
        w, wk = W(ti, l, 4)
        wvv = w.rearrange("p (k c) -> p k c", k=8)
        for ci, (c0, Lc) in enumerate(chunks):
            bv = P.bank()
            for kt in range(8):
                mm(ps[:Lc, bv, :512], zT[:, kt, c0:c0 + Lc], wvv[:, kt, :], kt == 0, kt == 7, [wk, zkeys[kt]], [PS(bv)])
            P.op("act", lambda e, bv=bv, ci=ci, Lc=Lc: e.copy(vtok[:Lc, ci, :], ps[:Lc, bv, :512]), reads=[PS(bv)], writes=[("vtok", ci)])

        dump("uT", uT[:], [128, 4, 512], BF16, [("uT", m) for m in range(4)], ti, l)
        dump("qk", qk[:], [64, 8, 512], BF16, [("qk", m) for m in range(8)], ti, l)
        dump("kst", kst[:], [64, 8, 256], BF16, [("kst", m) for m in range(8)], ti, l)
        dump("vtok", vtok[:], [64, 8, 512], BF16, [("vtok", m) for m in range(8)], ti, l)
        dump("srT", srT[:], [128, 4, 512], BF16, [("srT", m) for m in range(4)], ti, l)
        dump("dec", dec[:], [64, 8, 4], F32, [("dec", m) for m in range(8)], ti, l)
        P.tag = "gla_core"
        Lc0 = chunks[0][1]
        for h in range(4):
            bs = P.bank()
            for ci, (c0, Lc) in enumerate(chunks):
                mm(ps[:Lc, bs, 64 * ci:64 * ci + Lc], qk[:, 4 + h, c0:c0 + Lc], qk[:, h, c0:c0 + Lc], True, True,
                   [("qk", 4 + h), ("qk", h)], [PS(bs)])
            pv = ps[:Lc0, bs, 0:64 * nch].rearrange("p (c t) -> p c t", t=64)[:, :, :Lc0]
            cm = AP(causal, 0, [[64, Lc0], [0, nch], [1, Lc0]])
            P.op("dve", lambda e, h=h, pv=pv, cm=cm: e.tensor_tensor(out=scm[h][:Lc0, :nch, :Lc0], in0=pv, in1=cm, op=ALU.mult),
                 reads=[PS(bs), "causal"], writes=[("scm", h)])
        for h in range(4):
            x = h % 2
            kvb = []
            for g0_ in range(0, nch, 4):
                bkv = P.bank()
                kvb.append(bkv)
                gl = min(4, nch - g0_)
                for ci in range(g0_, g0_ + gl):
                    c0, Lc = chunks[ci]
                    mm(ps[:64, bkv, 128 * (ci - g0_):128 * (ci - g0_) + 128], kst[:Lc, ci, 64 * h:64 * h + 64],
                       vtok[:Lc, ci, 128 * h:128 * h + 128], True, True, [("kst", ci), ("vtok", ci)], [PS(bkv)])
            P.op("dve", lambda e, x=x, h=h: e.tensor_copy(Sall[x][:, 0, :], Scar[:, l, h, :]), reads=[("Scar", l, h)], writes=[("Sall", x)])
            for ci in range(nch):
                bkv = kvb[ci // 4]
                P.op("dve", lambda e, x=x, ci=ci, h=h, bkv=bkv: e.scalar_tensor_tensor(
                    out=Sall[x][:, ci + 1, :], in0=Sall[x][:, ci, :], scalar=dec[:, ci, h:h + 1],
                    in1=ps[:64, bkv, 128 * (ci % 4):128 * (ci % 4) + 128], op0=ALU.mult, op1=ALU.add),
                     reads=[("Sall", x), ("dec", ci), PS(bkv)], writes=[("Sall", x)])
            P.op("dve", lambda e, x=x, h=h: e.tensor_copy(Scar[:, l, h, :], Sall[x][:, nch, :]), reads=[("Sall", x)], writes=[("Scar", l, h)])
            P.op("act", lambda e, x=x, h=h: e.copy(Sbf[h][:, :nch, :], Sall[x][:, :nch, :]), reads=[("Sall", x)], writes=[("Sbf", h)])
        for h in range(4):
            bo = P.bank()
            for ci, (c0, Lc) in enumerate(chunks):
                mm(ps[:, bo, c0:c0 + Lc], vtok[:Lc, ci, 128 * h:128 * h + 128], scm[h][:Lc, ci, :Lc], True, False,
                   [("vtok", ci), ("scm", h)], [PS(bo)])
                mm(ps[:, bo, c0:c0 + Lc], Sbf[h][:, ci, :], qk[:, h, c0:c0 + Lc], False, True, [("Sbf", h), ("qk", h)], [PS(bo)])
            P.op("act", lambda e, bo=bo: e.activation(out=hid[:, 8, :N], in_=ps[:, bo, :N], func=AF.Square), reads=[PS(bo)], writes=[("hid", 8)])
            bm = P.bank()
            mm(ps[:, bm, :N], onesH[:], hid[:, 8, :N], True, True, [("hid", 8), "onesH"], [PS(bm)])
            P.op("act", lambda e, bm=bm: e.activation(out=rstd[:, :N], in_=ps[:, bm, :N], func=AF.Ln, bias=cst[:, 0:1]),
                 reads=[PS(bm), "cst"], writes=["rstd"])
            P.op("act", lambda e: e.activation(out=rstd[:, :N], in_=rstd[:, :N], func=AF.Exp, scale=-0.5), reads=["rstd"], writes=["rstd"])
            P.op("dve", lambda e, bo=bo, h=h: e.scalar_tensor_tensor(out=ftmp[0][:, :N], in0=ps[:, bo, :N], scalar=smalls[:, l, 24 + h:25 + h],
                                                                     in1=rstd[:, :N], op0=ALU.mult, op1=ALU.mult),
                 reads=[PS(bo), "rstd", "smalls"], writes=["ftmp0"])
            P.op("dve", lambda e, h=h: e.tensor_tensor(out=onT[:, h, :N], in0=ftmp[0][:, :N], in1=srT[:, h, :N], op=ALU.mult),
                 reads=["ftmp0", ("srT", h)], writes=[("onT", h)])

        dump("onT", onT[:], [128, 4, 512], BF16, [("onT", m) for m in range(4)], ti, l)
        P.tag = "yb"
        wpb, kpb = W(ti, l, 5)
        wpbv = wpb.rearrange("p (k c) -> p k c", k=4)
        wg = [W(ti, l, 6, keep=1), W(ti, l, 7, keep=2)]
        for m in range(8):
            wgv = wg[m // 4][0].rearrange("p (k c) -> p k c", k=8)
            wgk = wg[m // 4][1]
            bg = P.bank()
            for kt in range(8):
                mm(ps[:, bg, :N], wgv[:, kt, 128 * (m % 4):128 * (m % 4) + 128], zT[:, kt, :N], kt == 0, kt == 7, [wgk, zkeys[kt]], [PS(bg)])
            P.op("act", lambda e, bg=bg: e.activation(out=ftmp[1][:, :N], in_=ps[:, bg, :N], func=AF.Sigmoid), reads=[PS(bg)], writes=["ftmp1"])
            by = P.bank()
            for kt in range(4):
                mm(ps[:, by, :N], wpbv[:, kt, 128 * m:128 * m + 128], onT[:, kt, :N], kt == 0, kt == 3, [kpb, ("onT", kt)], [PS(by)])
            P.op("dve", lambda e, by=by, m=m: e.tensor_tensor(out=mixT[:, m, :N], in0=ps[:, by, :N], in1=ftmp[1][:, :N], op=ALU.mult),
                 reads=[PS(by), "ftmp1"], writes=[("mixT", m)])

        dump("mixb", mixT[:], [128, 8, 512], BF16, [("mixT", m) for m in range(8)], ti, l)
        dump("Gs", Gs[:], [128, 65, 2, 16], F32, ["Gs"], ti, l)
        P.tag = "s5_D"
        KTw, KTk = W(ti, l, 10)
        KTv = KTw.rearrange("p (c i s) -> p c i s", c=4, i=8)
        CA = [W(ti, l, 11, keep=1), W(ti, l, 12, keep=2)]
        for c in range(4):
            for j in range(R):
                by = P.bank()
                for i in range(j + 1):
                    rhs = uT[:, c, :N].rearrange("p (n s) -> p s n", s=R)[:, i, :]
                    mm(ps[:, by, :nb], KTv[:, c, j - i, :], rhs, i == 0, False, [KTk, ("uT", c)], [PS(by)])
                for part in range(2):
                    cav = CA[part][0].rearrange("p (q j s) -> p q j s", q=16, j=8)
                    for r in range(4):
                        kw = {"tile_position": (0, 96)} if r == 3 else {}
                        mm(ps[32 * r:32 * r + 32, by, :nb], cav[:, 4 * c + r, j, :], Gbf[:, part, 4 * c + r, :nb], False,
                           (r == 3 and part == 1), [CA[part][1]] + GBK, [PS(by)], **kw)
                uo = uT[:, c, :N].rearrange("p (n s) -> p s n", s=R)[:, j, :]
                yo = yT[:, c, :N].rearrange("p (n s) -> p s n", s=R)[:, j, :]
                P.op("dve", lambda e, by=by, uo=uo, yo=yo, c=c: e.scalar_tensor_tensor(out=yo, in0=uo, scalar=smalls[:, l, 16 + c:17 + c],
                                                                                       in1=ps[:, by, :nb], op0=ALU.mult, op1=ALU.add),
                     reads=[PS(by), ("uT", c), "smalls"], writes=[("yT", c)])
        for c in range(4):
            P.op("act", lambda e, c=c: e.activation(out=yT[:, c, :N], in_=yT[:, c, :N], func=AF.Gelu_apprx_tanh),
                 reads=[("yT", c)], writes=[("yT", c)])
            P.op("pool", lambda e, c=c: e.tensor_copy(hid[:, 9 + c, :N], yT[:, c, :N]), reads=[("yT", c)], writes=[("hid", 9 + c)])
        dump("act", yT[:], [128, 4, 512], F32, [("yT", m) for m in range(4)], ti, l)
        P.tag = "glu"
        wgl, kgl = W(ti, l, 13)
        wglv = wgl[:, 0:2048].rearrange("p (k c) -> p k c", k=4)
        for m in range(4):
            b = P.bank()
            for kt in range(4):
                mm(ps[:, b, :N], wglv[:, kt, 128 * m:128 * m + 128], hid[:, 9 + kt, :N], kt == 0, kt == 3, [kgl, ("hid", 9 + kt)], [PS(b)])
            P.op("act", lambda e, b=b, m=m: e.activation(out=ftmp[1][:, :N], in_=ps[:, b, :N], func=AF.Sigmoid,
                                                         bias=smalls[:, l, 20 + m:21 + m]), reads=[PS(b), "smalls"], writes=["ftmp1"])
            P.op("dve", lambda e, m=m: e.tensor_tensor(out=hid[:, 13 + m, :N], in0=yT[:, m, :N], in1=ftmp[1][:, :N], op=ALU.mult),
                 reads=[("yT", m), "ftmp1"], writes=[("hid", 13 + m)])
        dump("s5o", hid[:, 13:17, :], [128, 4, 512], BF16, [("hid", 13 + m) for m in range(4)], ti, l)
        P.tag = "ya"
        wpa, kpa = W(ti, l, 14)
        wpav = wpa.rearrange("p (k c) -> p k c", k=4)
        wg = [W(ti, l, 15, keep=1), W(ti, l, 16, keep=2)]
        for m in range(8):
            wgv = wg[m // 4][0].rearrange("p (k c) -> p k c", k=8)
            wgk = wg[m // 4][1]
            bg = P.bank()
            for kt in range(8):
                mm(ps[:, bg, :N], wgv[:, kt, 128 * (m % 4):128 * (m % 4) + 128], zT[:, kt, :N], kt == 0, kt == 7, [wgk, zkeys[kt]], [PS(bg)])
            P.op("act", lambda e, bg=bg: e.activation(out=ftmp[1][:, :N], in_=ps[:, bg, :N], func=AF.Sigmoid), reads=[PS(bg)], writes=["ftmp1"])
            by = P.bank()
            for kt in range(4):
                mm(ps[:, by, :N], wpav[:, kt, 128 * m:128 * m + 128], hid[:, 13 + kt, :N], kt == 0, kt == 3, [kpa, ("hid", 13 + kt)], [PS(by)])
            P.op("dve", lambda e, by=by: e.tensor_tensor(out=ftmp[0][:, :N], in0=ps[:, by, :N], in1=ftmp[1][:, :N], op=ALU.mult),
                 reads=[PS(by), "ftmp1"], writes=["ftmp0"])
            P.op("dve", lambda e, m=m: e.tensor_tensor(out=mixT[:, m, :N], in0=ftmp[0][:, :N], in1=mixT[:, m, :N], op=ALU.add),
                 reads=["ftmp0", ("mixT", m)], writes=[("mixT", m)])
        dump("mix", mixT[:], [128, 8, 512], BF16, [("mixT", m) for m in range(8)], ti, l)
        P.tag = "outproj"
        wo = [W(ti, l, 17), W(ti, l, 18, keep=1)]
        for m in range(8):
            wov = wo[m // 4][0].rearrange("p (k c) -> p k c", k=8)
            b = P.bank()
            for kt in range(8):
                mm(ps[:, b, :N], wov[:, kt, 128 * (m % 4):128 * (m % 4) + 128], mixT[:, kt, :N], kt == 0, kt == 7,
                   [wo[m // 4][1], ("mixT", kt)], [PS(b)])
            P.op("dve", lambda e, b=b, m=m: e.tensor_tensor(out=hT[:, m, :N], in0=hT[:, m, :N], in1=ps[:, b, :N], op=ALU.add),
                 reads=[PS(b), ("hT", m)], writes=[("hT", m)])
        dump("hmid", hT[:], [128, 8, 512], F32, [("hT", m) for m in range(8)], ti, l)
        P.tag = "ffn13"
        rmsnorm(N, lambda dt: smalls[:, l, 8 + dt:9 + dt], zT, "zT")
        for i in range(11):
            w, wk = W(ti, l, 19 + i)
            wv = w.rearrange("p (a k c) -> p a k c", a=2, k=8)
            for f2 in range(2):
                f = 2 * i + f2
                b1 = P.bank()
                for kt in range(8):
                    mm(ps[:, b1, :N], wv[:, 0, kt, 128 * f2:128 * f2 + 128], zT[:, kt, :N], kt == 0, kt == 7, [wk, zkeys[kt]], [PS(b1)])
                b3 = P.bank()
                for kt in range(8):
                    mm(ps[:, b3, :N], wv[:, 1, kt, 128 * f2:128 * f2 + 128], zT[:, kt, :N], kt == 0, kt == 7, [wk, zkeys[kt]], [PS(b3)])
                x = f % 2
                P.op("act", lambda e, b1=b1, x=x: e.activation(out=ftmp[x][:, :N], in_=ps[:, b1, :N], func=AF.Silu),
                     reads=[PS(b1)], writes=[f"ftmp{x}"])
                P.op("dve", lambda e, b3=b3, x=x, f=f: e.tensor_tensor(out=hid[:, f, :N], in0=ftmp[x][:, :N], in1=ps[:, b3, :N], op=ALU.mult),
                     reads=[PS(b3), f"ftmp{x}"], writes=[("hid", f)])
        P.tag = "ffn2"
        for m in range(8):
            w, wk = W(ti, l, 30 + m)
            wv = w[:, 0:22 * 128].rearrange("p (f c) -> p f c", f=22)
            b = P.bank()
            for f in range(22):
                mm(ps[:, b, :N], wv[:, f, :], hid[:, f, :N], f == 0, f == 21, [wk, ("hid", f)], [PS(b)])
            P.op("dve", lambda e, b=b, m=m: e.tensor_tensor(out=hT[:, m, :N], in0=hT[:, m, :N], in1=ps[:, b, :N], op=ALU.add),
                 reads=[PS(b), ("hT", m)], writes=[("hT", m)])
        dump("hend", hT[:], [128, 8, 512], F32, [("hT", m) for m in range(8)], ti, l)
        P.tag = "final"
        if l == L - 1 and ti > 0:
            rmsnorm(N, lambda dt: gf[:, dt:dt + 1], None, "yout", final_store=(t0 - NM))

    for ti in range(len(tiles)):
        for l in range(L):
            tile_layer(ti, l)
    P.barrier()
    P.emit()
    nc._prog = P
    return nc


_TILES_FULL = [(0, 16)] + [(16 + 512 * i, 512) for i in range(8)]


def make_in_map(inp, b, L, T):
    xT = np.empty((D, T), np.float32)
    xT[:, :NM] = inp["meta"].T
    xT[:, NM:] = inp["x"][b][:T - NM].T
    return {"xT": xT}


def shared_maps(inp, L):
    wpack = np.stack([pack_layer_weights(inp, l) for l in range(L)])
    s5 = [pack_s5(inp, l) for l in range(L)]
    sm, gf, walp = pack_small(inp, L)
    return {"wpack": wpack, "s5lam": np.stack([s[0] for s in s5]), "s5b": np.stack([s[1] for s in s5]),
            "s5c": np.stack([s[2] for s in s5]), "small": sm, "gfin": gf, "walp": walp}


def kernel(**inputs):
    inp = {k: np.asarray(v, dtype=np.float32) for k, v in inputs.items()}
    L = inp["w_in"].shape[0]
    B, SEQ, _ = inp["x"].shape
    T = SEQ + NM
    tiles = [(0, 16)] + [(16 + 512 * i, 512) for i in range(SEQ // 512)]
    import time as _t
    _t0 = _t.time()
    nc = build_nc(L, tiles)
    print("build_nc s", _t.time() - _t0, flush=True)
    shared = shared_maps(inp, L)
    in_maps = []
    for b in range(B):
        m = dict(shared)
        m.update(make_in_map(inp, b, L, T))
        in_maps.append(m)
    _t0 = _t.time()
    res = run_bass_kernel_spmd(nc, in_maps, core_ids=list(range(B)))
    print("run s", _t.time() - _t0, flush=True)
    out = np.empty((B, SEQ, D), np.float32)
    for b in range(B):
        out[b] = res.results[b]["outT"].T
    if DBG["on"]:
        DBG["res"] = {k: np.asarray(res.results[0][k]) for k in DBG["names"]}
    return out
```

```python
import numpy as np
from contextlib import ExitStack
import concourse.bass as bass
import concourse.mybir as mybir
from concourse.bass_utils import run_bass_kernel_spmd

F32 = mybir.dt.float32
BF16 = mybir.dt.bfloat16
AF = mybir.ActivationFunctionType
ALU = mybir.AluOpType
AP = bass.AP

D = 1024
NM = 16
R = 8
EPS = 1e-6
NCH = 38
CHW = 4096
DERIVED = (8, 9, 10, 11, 12)
HOST_CH = [c for c in range(NCH) if c not in DERIVED]
NS = 6
CH = 8000
NDMASEM = 8
ENGS = ("pe", "dve", "act", "pool", "sp")
SYNC_SAME = {"dve", "act"}


class Prog:
    def __init__(self, nc):
        self.nc = nc
        self.ops = {e: [] for e in ENGS}
        self.nops = {e: 0 for e in ENGS}
        self.signaled = {e: set() for e in ENGS}
        self.sems = {}
        self.seen = {e: {} for e in ENGS}
        self.lastw = {}
        self.readers = {}
        self.dma_n = {e: 0 for e in ENGS}
        self.nbank = 0
        self.held = set()
        self.tag = "init"
        self.tags = {e: [] for e in ENGS}

    def bank(self, hold=False):
        while True:
            b = self.nbank % 8
            self.nbank += 1
            if b not in self.held:
                break
        if hold:
            self.held.add(b)
        return b

    def release(self, b):
        self.held.discard(b)

    def _sem(self, prod, idx):
        k = (prod, idx)
        if k not in self.sems:
            self.sems[k] = self.nc.alloc_semaphore(name=f"s_{prod}_{idx}")
        return self.sems[k]

    def _ref(self, eng, p, t, waits):
        if self.seen[eng].get(p, 0) >= t:
            return
        self.seen[eng][p] = t
        if not p.startswith("dma:"):
            self.signaled[p].add(t)
        waits.append((p, t))

    def _deps(self, eng, reads, writes, force_same=False):
        need = {}

        def add(pt):
            if pt is None:
                return
            p, t = pt
            if need.get(p, 0) < t:
                need[p] = t
        for k in reads:
            add(self.lastw.get(k))
        for k in writes:
            add(self.lastw.get(k))
            for pt in self.readers.get(k, ()):
                add(pt)
        waits = []
        for p, t in need.items():
            if p == eng and not (eng in SYNC_SAME or force_same):
                continue
            self._ref(eng, p, t, waits)
        return waits

    def _commit(self, me, reads, writes):
        for k in reads:
            lst = self.readers.setdefault(k, [])
            lst[:] = [pt for pt in lst if pt[0] != me[0]]
            lst.append(me)
        for k in writes:
            self.lastw[k] = me
            self.readers[k] = []

    def op(self, eng, fn, reads=(), writes=()):
        waits = self._deps(eng, reads, writes)
        self.nops[eng] += 1
        t = self.nops[eng]
        self.ops[eng].append((waits, fn, ("op", t)))
        self.tags[eng].append(self.tag)
        self._commit((eng, t), reads, writes)

    def dma(self, eng, fn, reads=(), writes=()):
        n = self.dma_n[eng]
        self.dma_n[eng] += 1
        slot = n % NDMASEM
        gen = n // NDMASEM + 1
        prod = f"dma:{eng}/{slot}"
        waits = self._deps(eng, reads, writes, force_same=True)
        if gen > 1:
            self._ref(eng, prod, gen - 1, waits)
        self.ops[eng].append((waits, fn, ("dma", self._sem("dma_" + eng, slot))))
        self._commit((prod, gen), reads, writes)

    def barrier(self):
        last = {}
        for e in ENGS:
            if self.nops[e]:
                last[e] = self.nops[e]
            n = self.dma_n[e]
            for slot in range(min(n, NDMASEM)):
                last[f"dma:{e}/{slot}"] = (n - 1 - slot) // NDMASEM + 1
        for e in ENGS:
            waits = []
            for p, t in last.items():
                if p == e:
                    continue
                self._ref(e, p, t, waits)
            if waits:
                self.ops[e].append((waits, None, None))

    def emit(self):
        engmap = {"pe": "tensor", "dve": "vector", "act": "scalar", "pool": "gpsimd", "sp": "sync"}
        ticket = {}
        for e in ENGS:
            for k, idx in enumerate(sorted(self.signaled[e])):
                ticket[(e, idx)] = k + 1

        def semval(p, t):
            if p.startswith("dma:"):
                q, slot = p[4:].split("/")
                return (self._sem("dma_" + q, int(slot)), 16 * t)
            tk = ticket[(p, t)]
            return (self._sem(p, (tk - 1) // CH), (tk - 1) % CH + 1)
        with self.nc.Block() as block:
            for e in ENGS:
                ops = self.ops[e]

                def body(engobj, ops=ops, e=e):
                    for waits, fn, inc in ops:
                        for (p, t) in waits:
                            sv = semval(p, t)
                            engobj.wait_ge(sv[0], sv[1])
                        if fn is None:
                            continue
                        ins = fn(engobj)
                        if inc[0] == "dma":
                            ins.then_inc(inc[1], 16)
                        elif (e, inc[1]) in ticket:
                            tk = ticket[(e, inc[1])]
                            ins.then_inc(self._sem(e, (tk - 1) // CH), 1)
                getattr(block, engmap[e])(body)


def _lhsT_img(W, col0, ncols):
    K = W.shape[0]
    return np.ascontiguousarray(W[:, col0:col0 + ncols].reshape(K // 128, 128, ncols).transpose(1, 0, 2))


def pack_layer_weights(inp, l):
    out = np.zeros((len(HOST_CH), 128, CHW), np.float32)
    w_in = inp["w_in"][l]

    def put(c, img, off=0):
        flat = img.reshape(128, -1)
        out[HOST_CH.index(c), :, off:off + flat.shape[1]] = flat
    put(0, _lhsT_img(w_in, 0, 512))
    put(1, _lhsT_img(w_in, 512, 512))
    put(2, _lhsT_img(w_in, 1536, 512))
    put(3, _lhsT_img(w_in, 768, 256))
    put(3, _lhsT_img(w_in, 2048, 16), 2048)
    put(4, _lhsT_img(w_in, 1024, 512))
    put(5, _lhsT_img(inp["w_pb"][l], 0, 1024))
    put(6, _lhsT_img(w_in, 3088, 512))
    put(7, _lhsT_img(w_in, 3600, 512))
    put(13, _lhsT_img(inp["w_glu"][l], 0, 512))
    put(14, _lhsT_img(inp["w_pa"][l], 0, 1024))
    put(15, _lhsT_img(w_in, 2064, 512))
    put(16, _lhsT_img(w_in, 2576, 512))
    put(17, _lhsT_img(inp["w_out"][l], 0, 512))
    put(18, _lhsT_img(inp["w_out"][l], 512, 512))
    w1, w3, w2 = inp["w_ff1"][l], inp["w_ff3"][l], inp["w_ff2"][l]
    for i in range(11):
        img = np.stack([_lhsT_img(w1, 256 * i, 256), _lhsT_img(w3, 256 * i, 256)], axis=1)
        put(19 + i, img)
    for m in range(8):
        put(30 + m, _lhsT_img(w2, 128 * m, 128))
    return out


def pack_s5(inp, l):
    lam = np.zeros((128, 3, 16), np.float32)
    bS = np.zeros((128, 2, 16, 32), np.float32)
    cS = np.zeros((128, 2, 16, 32), np.float32)
    for q in range(16):
        for gg in range(2):
            g = 2 * q + gg
            sl = slice(gg * 64, gg * 64 + 64)
            lam[sl, 0, q] = inp["lam_re"][l, g]
            lam[sl, 1, q] = inp["lam_im"][l, g]
            lam[sl, 2, q] = inp["log_step"][l, g]
            bS[sl, 0, q, gg * 16:gg * 16 + 16] = inp["b_re"][l, g]
            bS[sl, 1, q, gg * 16:gg * 16 + 16] = inp["b_im"][l, g]
            cS[sl, 0, q, gg * 16:gg * 16 + 16] = inp["c_re"][l, g].T
            cS[sl, 1, q, gg * 16:gg * 16 + 16] = inp["c_im"][l, g].T
    return lam, bS, cS


def pack_small(inp, L):
    sm = np.zeros((128, L, 28), np.float32)
    for l in range(L):
        sm[:, l, 0:8] = inp["norm1"][l].reshape(8, 128).T
        sm[:, l, 8:16] = inp["norm2"][l].reshape(8, 128).T
        sm[:, l, 16:20] = inp["d_skip"][l].reshape(4, 128).T
        sm[:, l, 20:24] = inp["b_glu"][l].reshape(4, 128).T
        sm[:, l, 24:28] = inp["gla_norm"][l].reshape(4, 128).T
    gf = np.ascontiguousarray(inp["norm_f"].reshape(8, 128).T)
    walp = np.zeros((32, L, 256), np.float32)
    for l in range(L):
        walp[0:16, l] = inp["w_alpha"][l]
        walp[16, l] = inp["b_alpha"][l]
    return sm, gf, walp


DBG = {"on": False, "names": []}


def build_nc(L, tiles):
    T = tiles[-1][0] + tiles[-1][1]
    TOUT = T - NM
    nc = bass.Bass("TRN2", target_bir_lowering=False)
    P = Prog(nc)
    din = lambda n, s: nc.dram_tensor(n, s, F32, kind="ExternalInput").ap()
    xT = din("xT", [D, T])
    wpack = din("wpack", [L, len(HOST_CH), 128, CHW])
    s5lam = din("s5lam", [L, 128, 3, 16])
    s5b = din("s5b", [L, 128, 2, 16, 32])
    s5c = din("s5c", [L, 128, 2, 16, 32])
    small = din("small", [128, L, 28])
    gfin = din("gfin", [128, 8])
    walp_d = din("walp", [32, L, 256])
    outT = nc.dram_tensor("outT", [D, TOUT], F32, kind="ExternalOutput").ap()
    wbf = nc.dram_tensor("wbf", [L, NCH, 128, CHW], BF16, kind="Internal").ap()

    sb = lambda n, s, d=F32: nc.alloc_sbuf_tensor(n, s, d)
    ps = nc.alloc_psum_tensor("ps", [128, 8, 512], F32)
    PS = lambda b: ("ps", b)

    wring = sb("wring", [128, NS, CHW], BF16)
    ident = sb("ident", [128, 128])
    onesD = sb("onesD", [128, 128], BF16)
    onesH = sb("onesH", [128, 128], BF16)
    maskU = sb("maskU", [64, 64], BF16)
    maskL = sb("maskL", [64, 64], BF16)
    causal = sb("causal", [64, 64])
    cst = sb("cst", [128, 4])
    smalls = sb("smalls", [128, L, 28])
    gf = sb("gf", [128, 8])
    walp = sb("walp_s", [32, L, 256])
    A1 = sb("A1", [128, L, 2, 16])
    A2 = sb("A2", [128, L, 2, 16])
    Gcar = sb("Gcar", [128, L, 2, 16])
    Scar = sb("Scar", [64, L, 4, 128])
    tmpc = sb("tmpc", [128, 128])

    def v(e_, f):
        return f

    P.op("pool", lambda e: e.memset(tmpc[:], 1.0), writes=["tmpc"])
    P.op("pool", lambda e: e.affine_select(out=ident[:], in_=tmpc[:], pattern=[[1, 128]], compare_op=ALU.is_equal,
                                           fill=0.0, base=0, channel_multiplier=-1), reads=["tmpc"], writes=["ident"])
    P.op("pool", lambda e: e.memset(onesD[:], 1.0 / 1024), writes=["onesD"])
    P.op("pool", lambda e: e.memset(onesH[:], 1.0 / 128), writes=["onesH"])
    P.op("pool", lambda e: e.memset(cst[:, 0:1], EPS), writes=["cst"])
    P.op("pool", lambda e: e.memset(cst[:, 1:2], 1.0), writes=["cst"])
    P.op("pool", lambda e: e.memset(cst[:, 2:3], 0.0), writes=["cst"])
    P.op("pool", lambda e: e.affine_select(out=causal[:], in_=tmpc[0:64, 0:64], pattern=[[1, 64]], compare_op=ALU.is_ge,
                                           fill=0.0, base=0, channel_multiplier=-1), reads=["tmpc"], writes=["causal"])
    P.op("pool", lambda e: e.tensor_scalar(out=maskU[:], in0=causal[:], scalar1=-1.0 / 16, scalar2=None, op0=ALU.mult),
         reads=["causal"], writes=["maskU"])
    P.op("pool", lambda e: e.tensor_scalar(out=maskL[:], in0=causal[:], scalar1=1.0 / 16, scalar2=-1.0 / 16, op0=ALU.mult,
                                           op1=ALU.add), reads=["causal"], writes=["maskL"])
    P.dma("sp", lambda e: e.dma_start(out=smalls[:], in_=small[:, :, :]), writes=["smalls"])
    P.dma("sp", lambda e: e.dma_start(out=gf[:], in_=gfin[:, :]), writes=["gf"])
    P.dma("sp", lambda e: e.dma_start(out=walp[:], in_=walp_d[:, :, :]), writes=["walp"])
    P.op("pool", lambda e: e.memset(Gcar[:], 0.0), writes=["Gcar"])
    P.op("pool", lambda e: e.memset(Scar[:], 0.0), writes=["Scar"])

    n = 0
    for l in range(L):
        for ci, c in enumerate(HOST_CH):
            s = n % NS
            n += 1
            P.dma("pool", lambda e, l=l, ci=ci, s=s: e.dma_start(out=wring[:, s, :], in_=wpack[l, ci]),
                  writes=[("w", s)])
            P.dma("sp", lambda e, l=l, c=c, s=s: e.dma_start(out=wbf[l, c], in_=wring[:, s, :]),
                  reads=[("w", s)], writes=[("wbf", l, c)])

    with ExitStack() as es:
        tb = lambda nme, s, d=F32: es.enter_context(nc.sbuf_tensor(nme, s, d))
        lam = tb("p_lam", [128, 3, 16])
        bS = tb("p_bS", [128, 2, 16, 32])
        cS = tb("p_cS", [128, 2, 16, 32])
        ncSi = tb("p_ncSi", [128, 16, 32])
        sc = tb("p_sc", [128, 32, 16])
        Mr = [tb(f"p_Mr{i}", [128, 16, 32]) for i in range(2)]
        Mi = [tb(f"p_Mi{i}", [128, 16, 32]) for i in range(2)]
        t32 = [tb(f"p_t32{i}", [128, 16, 32]) for i in range(2)]
        BD = tb("p_BD", [128, 128])
        stg = {c: tb(f"p_stg{c}", [128, CHW], BF16) for c in DERIVED}
        P.op("dve", lambda e: e.memset(BD[:], 0.0), writes=["BD"])
        for r in range(4):
            P.op("dve", lambda e, r=r: e.memset(BD[32 * r:32 * r + 32, 32 * r:32 * r + 32], 1.0), reads=["BD"], writes=["BD"])
        WBv = {0: stg[8][:].rearrange("p (c i s) -> p c i s", c=4, i=8), 1: stg[9][:].rearrange("p (c i s) -> p c i s", c=4, i=8)}
        KTv = stg[10][:].rearrange("p (c i s) -> p c i s", c=4, i=8)
        CAv = {0: stg[11][:].rearrange("p (q j s) -> p q j s", q=16, j=8), 1: stg[12][:].rearrange("p (q j s) -> p q j s", q=16, j=8)}
        S = lambda i: sc[:, i, :]
        K = lambda i: ("sc", i)

        def bc(a16):
            return AP(a16.tensor, a16.offset, [list(a16.ap[0]), list(a16.ap[1]), [0, 32]])

        def tt(out, a, b, op, r, w, eng="dve"):
            P.op(eng, lambda e: e.tensor_tensor(out=out, in0=a, in1=b, op=op), reads=r, writes=w)

        def ts(out, a, s1, s2, op0, op1, r, w, eng="dve"):
            if op1 is None:
                P.op(eng, lambda e: e.tensor_scalar(out=out, in0=a, scalar1=s1, scalar2=None, op0=op0), reads=r, writes=w)
            else:
                P.op(eng, lambda e: e.tensor_scalar(out=out, in0=a, scalar1=s1, scalar2=s2, op0=op0, op1=op1), reads=r, writes=w)

        def act(out, a, fn, r, w, **kw):
            P.op("act", lambda e: e.activation(out=out, in_=a, func=fn, **kw), reads=r, writes=w)

        TWO_PI = 2.0 * np.pi
        C1 = 6.28125
        C2 = TWO_PI - C1
        MAGIC = 12582912.0

        def sin_of(dst, src_i, shift, ktmp):
            y, t1, t2 = ktmp
            ts(S(y), S(src_i), shift, None, ALU.add, None, [K(src_i)], [K(y)])
            ts(S(t1), S(y), 1.0 / TWO_PI, None, ALU.mult, None, [K(y)], [K(t1)])
            ts(S(t2), S(t1), MAGIC, None, ALU.add, None, [K(t1)], [K(t2)])
            ts(S(t1), S(t2), -MAGIC, None, ALU.add, None, [K(t2)], [K(t1)])
            P.op("dve", lambda e: e.scalar_tensor_tensor(out=S(t2), in0=S(t1), scalar=-C1, in1=S(y), op0=ALU.mult, op1=ALU.add),
                 reads=[K(t1), K(y)], writes=[K(t2)])
            P.op("dve", lambda e: e.scalar_tensor_tensor(out=S(y), in0=S(t1), scalar=-C2, in1=S(t2), op0=ALU.mult, op1=ALU.add),
                 reads=[K(t1), K(t2)], writes=[K(y)])
            ts(S(y), S(y), float(np.pi), float(-np.pi), ALU.min, ALU.max, [K(y)], [K(y)])
            act(S(dst), S(y), AF.Sin, [K(y)], [K(dst)])

        for l in range(L):
            P.dma("sp", lambda e, l=l: e.dma_start(out=lam[:], in_=s5lam[l]), writes=["lam"])
            P.dma("sp", lambda e, l=l: e.dma_start(out=bS[:], in_=s5b[l]), writes=["bS"])
            P.dma("sp", lambda e, l=l: e.dma_start(out=cS[:], in_=s5c[l]), writes=["cS"])
            ts(ncSi[:], cS[:, 1], -1.0, None, ALU.mult, None, ["cS"], ["ncSi"])
            STEP, LR, LI, MAG, ANG, COS, SIN, ARE, AIM, DEN, NR, CFR, CFI, T0, T1, T2, T3, PR, PI_, PR2, PI2 = range(21)
            act(S(STEP), lam[:, 2, :], AF.Exp, ["lam"], [K(STEP)])
            ts(S(LR), lam[:, 0, :], -1e-4, None, ALU.min, None, ["lam"], [K(LR)])
            ts(S(LI), lam[:, 1, :], 1.0, None, ALU.mult, None, ["lam"], [K(LI)])
            tt(S(T0), S(LR), S(STEP), ALU.mult, [K(LR), K(STEP)], [K(T0)])
            act(S(MAG), S(T0), AF.Exp, [K(T0)], [K(MAG)])
            tt(S(ANG), S(LI), S(STEP), ALU.mult, [K(LI), K(STEP)], [K(ANG)])
            sin_of(SIN, ANG, 0.0, (T1, T2, T3))
            sin_of(COS, ANG, float(np.pi / 2), (T1, T2, T3))
            tt(S(ARE), S(MAG), S(COS), ALU.mult, [K(MAG), K(COS)], [K(ARE)])
            tt(S(AIM), S(MAG), S(SIN), ALU.mult, [K(MAG), K(SIN)], [K(AIM)])
            tt(S(T0), S(LR), S(LR), ALU.mult, [K(LR)], [K(T0)])
            tt(S(T1), S(LI), S(LI), ALU.mult, [K(LI)], [K(T1)])
            tt(S(DEN), S(T0), S(T1), ALU.add, [K(T0), K(T1)], [K(DEN)])
            P.op("dve", lambda e: e.reciprocal(S(DEN), S(DEN)), reads=[K(DEN)], writes=[K(DEN)])
            ts(S(NR), S(ARE), -1.0, None, ALU.add, None, [K(ARE)], [K(NR)])
            tt(S(T0), S(NR), S(LR), ALU.mult, [K(NR), K(LR)], [K(T0)])
            tt(S(T1), S(AIM), S(LI), ALU.mult, [K(AIM), K(LI)], [K(T1)])
            tt(S(T0), S(T0), S(T1), ALU.add, [K(T0), K(T1)], [K(T0)])
            tt(S(CFR), S(T0), S(DEN), ALU.mult, [K(T0), K(DEN)], [K(CFR)])
            tt(S(T0), S(AIM), S(LR), ALU.mult, [K(AIM), K(LR)], [K(T0)])
            tt(S(T1), S(NR), S(LI), ALU.mult, [K(NR), K(LI)], [K(T1)])
            tt(S(T0), S(T0), S(T1), ALU.subtract, [K(T0), K(T1)], [K(T0)])
            tt(S(CFI), S(T0), S(DEN), ALU.mult, [K(T0), K(DEN)], [K(CFI)])

            def cmul(dr, di, ar_i, ai_i, xr, xi, rk, wk):
                tt(t32[0][:], xr, bc(S(ar_i)), ALU.mult, rk + [K(ar_i)], ["t32a"])
                tt(t32[1][:], xi, bc(S(ai_i)), ALU.mult, rk + [K(ai_i)], ["t32b"])
                tt(dr, t32[0][:], t32[1][:], ALU.subtract, ["t32a", "t32b"], [wk[0]])
                tt(t32[0][:], xi, bc(S(ar_i)), ALU.mult, rk + [K(ar_i)], ["t32a"])
                tt(t32[1][:], xr, bc(S(ai_i)), ALU.mult, rk + [K(ai_i)], ["t32b"])
                tt(di, t32[0][:], t32[1][:], ALU.add, ["t32a", "t32b"], [wk[1]])

            cur = 0
            cmul(Mr[0][:], Mi[0][:], CFR, CFI, bS[:, 0], bS[:, 1], ["bS"], [("M", 0, 0), ("M", 0, 1)])
            for tau in range(R):
                mk = [("M", cur, 0), ("M", cur, 1)]
                for c in range(4):
                    b = P.bank()
                    mr = Mr[cur][:, 4 * c:4 * c + 4, :].rearrange("p a b -> p (a b)")
                    mi = Mi[cur][:, 4 * c:4 * c + 4, :].rearrange("p a b -> p (a b)")
                    cr = cS[:, 0, 4 * c:4 * c + 4, :].rearrange("p a b -> p (a b)")
                    nci = ncSi[:, 4 * c:4 * c + 4, :].rearrange("p a b -> p (a b)")
                    P.op("pe", lambda e, b=b, mr=mr, cr=cr: e.matmul(ps[:, b, 0:128], mr, cr, start=True, stop=False),
                         reads=[mk[0], "cS"], writes=[PS(b)])
                    P.op("pe", lambda e, b=b, mi=mi, nci=nci: e.matmul(ps[:, b, 0:128], mi, nci, start=False, stop=True),
                         reads=[mk[1], "ncSi"], writes=[PS(b)])
                    P.op("dve", lambda e, b=b, c=c, tau=tau: e.tensor_tensor(out=KTv[:, c, tau, :], in0=ps[:, b, 0:128], in1=BD[:],
                                                                               op=ALU.mult), reads=[PS(b), "BD"], writes=[("stg", 10)])
                    for part, Mx in ((0, mr), (1, mi)):
                        b2 = P.bank()
                        P.op("pe", lambda e, b2=b2, Mx=Mx: e.transpose(ps[:, b2, 0:128], Mx, ident[:]),
                             reads=[mk[part], "ident"], writes=[PS(b2)])
                        P.op("act", lambda e, b2=b2, part=part, c=c, tau=tau: e.copy(WBv[part][:, c, R - 1 - tau, :], ps[:, b2, 0:128]),
                             reads=[PS(b2)], writes=[("stg", 8 + part)])
                if tau < R - 1:
                    nxt = 1 - cur
                    cmul(Mr[nxt][:], Mi[nxt][:], ARE, AIM, Mr[cur][:], Mi[cur][:], mk, [("M", nxt, 0), ("M", nxt, 1)])
                    cur = nxt
            ts(S(PR), S(ARE), 1.0, None, ALU.mult, None, [K(ARE)], [K(PR)])
            ts(S(PI_), S(AIM), 1.0, None, ALU.mult, None, [K(AIM)], [K(PI_)])
            pr, pi, pr2, pi2 = PR, PI_, PR2, PI2
            for j in range(R):
                tt(t32[0][:], cS[:, 0], bc(S(pr)), ALU.mult, ["cS", K(pr)], ["t32a"])
                tt(t32[1][:], cS[:, 1], bc(S(pi)), ALU.mult, ["cS", K(pi)], ["t32b"])
                tt(CAv[0][:, :, j, :], t32[0][:], t32[1][:], ALU.subtract, ["t32a", "t32b"], [("stg", 11)])
                tt(t32[0][:], ncSi[:], bc(S(pr)), ALU.mult, ["ncSi", K(pr)], ["t32a"])
                tt(t32[1][:], cS[:, 0], bc(S(pi)), ALU.mult, ["cS", K(pi)], ["t32b"])
                tt(CAv[1][:, :, j, :], t32[0][:], t32[1][:], ALU.subtract, ["t32a", "t32b"], [("stg", 12)])
                if j < R - 1:
                    tt(S(T0), S(pr), S(ARE), ALU.mult, [K(pr), K(ARE)], [K(T0)])
                    tt(S(T1), S(pi), S(AIM), ALU.mult, [K(pi), K(AIM)], [K(T1)])
                    tt(S(pr2), S(T0), S(T1), ALU.subtract, [K(T0), K(T1)], [K(pr2)])
                    tt(S(T0), S(pr), S(AIM), ALU.mult, [K(pr), K(AIM)], [K(T0)])
                    tt(S(T1), S(pi), S(ARE), ALU.mult, [K(pi), K(ARE)], [K(T1)])
                    tt(S(pi2), S(T0), S(T1), ALU.add, [K(T0), K(T1)], [K(pi2)])
                    pr, pi, pr2, pi2 = pr2, pi2, pr, pi
            for h2 in range(2):
                ts(A1[:, l, h2, :], S(pr), 1.0, None, ALU.mult, None, [K(pr)], ["A1"])
            ts(A2[:, l, 0, :], S(pi), -1.0, None, ALU.mult, None, [K(pi)], ["A2"])
            ts(A2[:, l, 1, :], S(pi), 1.0, None, ALU.mult, None, [K(pi)], ["A2"])
            for c in DERIVED:
                P.dma("sp", lambda e, l=l, c=c: e.dma_start(out=wbf[l, c], in_=stg[c][:]), reads=[("stg", c)],
                      writes=[("wbf", l, c)])
        P.barrier()

    NT = 512
    hT = sb("hT", [128, 8, NT])
    zT = sb("zT", [128, 8, NT], BF16)
    hid = sb("hid", [128, 22, NT], BF16)
    rstd = sb("rstd", [128, NT])
    uT = sb("uT", [128, 4, NT], BF16)
    qk = sb("qk", [64, 8, NT], BF16)
    srT = sb("srT", [128, 4, NT], BF16)
    alT = sb("alT", [32, NT])
    mixT = sb("mixT", [128, 8, NT], BF16)
    ftmp = [sb(f"ftmp{i}", [128, NT]) for i in range(2)]
    vtok = sb("vtok", [64, 8, 512], BF16)
    kst = sb("kst", [64, 8, 256], BF16)
    SPt = [sb(f"SPt{i}", [64, 256], BF16) for i in range(2)]
    E3t = [sb(f"E3t{i}", [64, 256]) for i in range(2)]
    etm = [sb(f"etm{i}", [64, 256]) for i in range(2)]
    E12 = [sb(f"E12_{i}", [64, 2, 4, 64]) for i in range(2)]
    dec = sb("dec", [64, 8, 4])
    scm = [sb(f"scm{i}", [64, 8, 64], BF16) for i in range(4)]
    Sall = [sb(f"Sall{i}", [64, 9, 128]) for i in range(2)]
    Sbf = [sb(f"Sbf{i}", [64, 8, 128], BF16) for i in range(4)]
    onT = sb("onT", [128, 4, NT], BF16)
    Gs = sb("Gs", [128, 65, 2, 16])
    Gbf = hid[:, 17:21, :].rearrange("p m n -> p (m n)").rearrange("p (a q b) -> p a q b", a=2, q=16)
    GBK = [("hid", 17 + i) for i in range(4)]
    gt = [sb(f"gt{i}", [128, 2, 16]) for i in range(2)]
    yT = sb("yT", [128, 4, NT])

    P.op("pool", lambda e: e.memset(alT[:], 1.0), writes=["alT"])

    ORDER = [0, 8, 9, 1, 3, 4, 2] + [c for c in range(5, NCH) if c not in (8, 9)]
    sched = [(ti, l, c) for ti in range(len(tiles)) for l in range(L) for c in ORDER]
    wstate = {"issued": 0, "next": 0}

    def issue_loads(upto):
        while wstate["issued"] < min(upto, len(sched)):
            i = wstate["issued"]
            _, l, c = sched[i]
            s = i % NS
            P.dma("sp", lambda e, l=l, c=c, s=s: e.dma_start(out=wring[:, s, :], in_=wbf[l, c]),
                  reads=[("wbf", l, c)], writes=[("w", s)])
            wstate["issued"] += 1

    def W(ti, l, c, keep=0):
        i = wstate["next"]
        assert sched[i] == (ti, l, c), (sched[i], (ti, l, c))
        wstate["next"] += 1
        issue_loads(max(i + 1, i - keep + NS))
        s = i % NS
        return wring[:, s, :], ("w", s)

    def dump(name, tens, shape, dtype, keys, ti, l):
        if not DBG["on"] or not (ti == 1 and l == 0):
            return
        dd = nc.dram_tensor("dbg_" + name, list(shape), dtype, kind="ExternalOutput").ap()
        DBG["names"].append("dbg_" + name)
        P.dma("sp", lambda e: e.dma_start(out=dd, in_=tens), reads=keys, writes=["dbg_" + name])

    def mm(out, lhsT, rhs, start, stop, reads, writes, **kw):
        P.op("pe", lambda e: e.matmul(out, lhsT, rhs, start=start, stop=stop, **kw), reads=reads, writes=writes)

    def rmsnorm(N, gain_col, dst, dkey, final_store=None):
        for dt in range(8):
            P.op("act", lambda e, dt=dt: e.activation(out=hid[:, dt, :N], in_=hT[:, dt, :N], func=AF.Square),
                 reads=[("hT", dt)], writes=[("hid", dt)])
        b = P.bank()
        for dt in range(8):
            mm(ps[:, b, :N], onesD[:], hid[:, dt, :N], dt == 0, dt == 7, [("hid", dt), "onesD"], [PS(b)])
        P.op("act", lambda e: e.activation(out=rstd[:, :N], in_=ps[:, b, :N], func=AF.Ln, bias=cst[:, 0:1]),
             reads=[PS(b), "cst"], writes=["rstd"])
        P.op("act", lambda e: e.activation(out=rstd[:, :N], in_=rstd[:, :N], func=AF.Exp, scale=-0.5),
             reads=["rstd"], writes=["rstd"])
        if final_store is not None:
            for dt in range(8):
                x = dt % 2
                P.op("dve", lambda e, dt=dt, x=x: e.scalar_tensor_tensor(out=ftmp[x][:, :N], in0=hT[:, dt, :N], scalar=gain_col(dt),
                                                                         in1=rstd[:, :N], op0=ALU.mult, op1=ALU.mult),
                     reads=[("hT", dt), "rstd", "smalls", "gf"], writes=[f"ftmp{x}"])
                P.dma("sp", lambda e, dt=dt, x=x: e.dma_start(out=outT[128 * dt:128 * dt + 128, final_store:final_store + N],
                                                              in_=ftmp[x][:, :N]), reads=[f"ftmp{x}"], writes=[("outT", dt)])
            return
        for dt in range(8):
            P.op("dve", lambda e, dt=dt: e.scalar_tensor_tensor(out=dst[:, dt, :N], in0=hT[:, dt, :N], scalar=gain_col(dt),
                                                                 in1=rstd[:, :N], op0=ALU.mult, op1=ALU.mult),
                 reads=[("hT", dt), "rstd", "smalls", "gf"], writes=[(dkey, dt)])

    zkeys = [("zT", kt) for kt in range(8)]

    def tile_layer(ti, l):
        t0, N = tiles[ti]
        nb = N // R
        if N >= 64:
            chunks = [(64 * i, 64) for i in range(N // 64)]
        else:
            chunks = [(0, N)]
        nch = len(chunks)
        sm = lambda a, b_: smalls[:, l, a:b_]

        if l == 0:
            P.dma("sp", lambda e: e.dma_start(out=hT[:, :, :N], in_=xT.rearrange("(dt p) t -> p dt t", p=128)[:, :, t0:t0 + N]),
                  writes=[("hT", dt) for dt in range(8)])
        P.tag = "norm1"
        rmsnorm(N, lambda dt: smalls[:, l, dt:dt + 1], zT, "zT")

        dump("zT", zT[:], [128, 8, 512], BF16, zkeys, ti, l)
        P.tag = "inproj_u"
        w, wk = W(ti, l, 0)
        wv = w.rearrange("p (k c) -> p k c", k=8)
        for m in range(4):
            b = P.bank()
            for kt in range(8):
                mm(ps[:, b, :N], wv[:, kt, 128 * m:128 * m + 128], zT[:, kt, :N], kt == 0, kt == 7, [wk, zkeys[kt]], [PS(b)])
            P.op("act", lambda e, b=b, m=m: e.copy(uT[:, m, :N], ps[:, b, :N]), reads=[PS(b)], writes=[("uT", m)])
        P.tag = "s5_A"
        WB = [W(ti, l, 8), W(ti, l, 9, keep=1)]
        for c in range(4):
            for part in range(2):
                wbv = WB[part][0].rearrange("p (c i s) -> p c i s", c=4, i=8)
                bks = [P.bank() for _ in range(4)]
                for i in range(R):
                    for r in range(4):
                        kw = {"tile_position": (96, 0)} if r == 3 else {}
                        rhs = uT[32 * r:32 * r + 32, c, :N].rearrange("p (n s) -> p s n", s=R)[:, i, :]
                        mm(ps[:, bks[r], :nb], wbv[32 * r:32 * r + 32, c, i, :], rhs, i == 0, i == R - 1, [WB[part][1], ("uT", c)],
                           [PS(bks[r])], **kw)
                for r in range(4):
                    P.op("act", lambda e, be=bks[r], part=part, q=4 * c + r: e.copy(Gs[:, 1:nb + 1, part, q], ps[:, be, :nb]),
                         reads=[PS(bks[r])], writes=["Gs"])
        P.tag = "s5_scan"
        P.op("pool", lambda e: e.tensor_copy(Gs[:, 0, :, :], Gcar[:, l, :, :]), reads=["Gcar"], writes=["Gs"])
        for b_ in range(nb):
            prev = Gs[:, b_, :, :]
            prevsw = AP(Gs, (b_ * 32 + 16), [[65 * 32, 128], [-16, 2], [1, 16]])
            P.op("pool", lambda e, prev=prev: e.tensor_tensor(out=gt[0][:], in0=prev, in1=A1[:, l, :, :], op=ALU.mult),
                 reads=["Gs", "A1"], writes=["gt0"])
            P.op("pool", lambda e, prevsw=prevsw: e.tensor_tensor(out=gt[1][:], in0=prevsw, in1=A2[:, l, :, :], op=ALU.mult),
                 reads=["Gs", "A2"], writes=["gt1"])
            P.op("pool", lambda e: e.tensor_tensor(out=gt[0][:], in0=gt[0][:], in1=gt[1][:], op=ALU.add), reads=["gt0", "gt1"], writes=["gt0"])
            P.op("pool", lambda e, b_=b_: e.tensor_tensor(out=Gs[:, b_ + 1, :, :], in0=Gs[:, b_ + 1, :, :], in1=gt[0][:], op=ALU.add),
                 reads=["Gs", "gt0"], writes=["Gs"])
        P.op("pool", lambda e: e.tensor_copy(Gcar[:, l, :, :], Gs[:, nb, :, :]), reads=["Gs"], writes=["Gcar"])
        P.op("pool", lambda e: e.tensor_copy(Gbf[:, :, :, :nb].rearrange("p a q b -> p b (a q)"),
                                             Gs[:, 0:nb, :, :].rearrange("p b a q -> p b (a q)")), reads=["Gs"], writes=GBK)
        P.tag = "inproj_qk"
        w, wk = W(ti, l, 1)
        wv = w.rearrange("p (k c) -> p k c", k=8)
        for m in range(8):
            b = P.bank()
            for kt in range(8):
                mm(ps[:64, b, :N], wv[:, kt, 64 * m:64 * m + 64], zT[:, kt, :N], kt == 0, kt == 7, [wk, zkeys[kt]], [PS(b)])
            P.op("dve", lambda e, b=b, m=m: e.tensor_copy(qk[:, m, :N], ps[:64, b, :N]), reads=[PS(b)], writes=[("qk", m)])
        P.tag = "gla_gates"
        w, wk = W(ti, l, 3)
        wkt = w[:, 0:2048].rearrange("p (k c) -> p k c", k=8)
        wal = w[:, 2048:2176].rearrange("p (k c) -> p k c", k=8)
        b = P.bank()
        for kt in range(8):
            mm(ps[:16, b, :N], wal[:, kt, :], zT[:, kt, :N], kt == 0, kt == 7, [wk, zkeys[kt]], [PS(b)])
        P.op("dve", lambda e, b=b: e.tensor_copy(alT[0:16, :N], ps[:16, b, :N]), reads=[PS(b)], writes=["alT"])
        for ci, (c0, Lc) in enumerate(chunks):
            x = ci % 2
            bl = P.bank()
            mm(ps[:Lc, bl, :256], alT[0:17, c0:c0 + Lc], walp[0:17, l, :], True, True, ["alT", "walp"], [PS(bl)])
            P.op("act", lambda e, bl=bl, x=x, Lc=Lc: e.activation(out=etm[x][:Lc, :], in_=ps[:Lc, bl, :256], func=AF.Exp, scale=-1.0),
                 reads=[PS(bl)], writes=[("etm", x)])
            P.op("act", lambda e, x=x, Lc=Lc: e.activation(out=SPt[x][:Lc, :], in_=etm[x][:Lc, :], func=AF.Ln, bias=cst[:Lc, 1:2]),
                 reads=[("etm", x), "cst"], writes=[("SPt", x)])
            bb = P.bank()
            for h in range(4):
                mm(ps[:64, bb, 64 * h:64 * h + Lc], SPt[x][:Lc, 64 * h:64 * h + 64], maskU[:Lc, :Lc], True, True,
                   [("SPt", x), "maskU"], [PS(bb)])
            pv = ps[:64, bb, 0:256].rearrange("p (h t) -> p h t", h=4)[:, :, :Lc]
            P.op("act", lambda e, x=x, pv=pv, Lc=Lc: e.activation(out=E12[x][:, 0, :, :Lc], in_=pv, func=AF.Exp),
                 reads=[PS(bb)], writes=[("E12", x, 0)])
            P.op("act", lambda e, x=x, pv=pv, Lc=Lc: e.activation(out=E12[x][:, 1, :, :Lc], in_=pv, func=AF.Exp, scale=-1.0),
                 reads=[PS(bb)], writes=[("E12", x, 1)])
            P.op("dve", lambda e, x=x, c0=c0, Lc=Lc: e.scalar_tensor_tensor(out=qk[:, 0:4, c0:c0 + Lc], in0=qk[:, 0:4, c0:c0 + Lc],
                                                                            scalar=0.125, in1=E12[x][:, 0, :, :Lc], op0=ALU.mult,
                                                                            op1=ALU.mult),
                 reads=[("E12", x, 0)] + [("qk", m) for m in range(4)], writes=[("qk", m) for m in range(4)])
            P.op("dve", lambda e, x=x, c0=c0, Lc=Lc: e.tensor_tensor(out=qk[:, 4:8, c0:c0 + Lc], in0=qk[:, 4:8, c0:c0 + Lc],
                                                                     in1=E12[x][:, 1, :, :Lc], op=ALU.mult),
                 reads=[("E12", x, 1)] + [("qk", m) for m in range(4, 8)], writes=[("qk", m) for m in range(4, 8)])
            P.op("dve", lambda e, x=x, ci=ci, Lc=Lc: e.tensor_copy(dec[:, ci, :], E12[x][:, 0, :, Lc - 1]),
                 reads=[("E12", x, 0)], writes=[("dec", ci)])
            br = P.bank()
            mm(ps[:Lc, br, :256], maskL[:Lc, :Lc], SPt[x][:Lc, :], True, True, [("SPt", x), "maskL"], [PS(br)])
            P.op("act", lambda e, br=br, x=x, Lc=Lc: e.activation(out=E3t[x][:Lc, :], in_=ps[:Lc, br, :256], func=AF.Exp),
                 reads=[PS(br)], writes=[("E3t", x)])
            bk = P.bank()
            for kt in range(8):
                mm(ps[:Lc, bk, :256], zT[:, kt, c0:c0 + Lc], wkt[:, kt, :], kt == 0, kt == 7, [wk, zkeys[kt]], [PS(bk)])
            P.op("dve", lambda e, bk=bk, x=x, ci=ci, Lc=Lc: e.tensor_tensor(out=kst[:Lc, ci, :], in0=ps[:Lc, bk, :256], in1=E3t[x][:Lc, :],
                                                                            op=ALU.mult),
                 reads=[PS(bk), ("E3t", x)], writes=[("kst", ci)])
        P.tag = "v_tok"
        w, wk = W(ti, l, 4)
        wvv = w.rearrange("p (k c) -> p k c", k=8)
        for ci, (c0, Lc) in enumerate(chunks):
            bv = P.bank()
            for kt in range(8):
                mm(ps[:Lc, bv, :512], zT[:, kt, c0:c0 + Lc], wvv[:, kt, :], kt == 0, kt == 7, [wk, zkeys[kt]], [PS(bv)])
            P.op("act", lambda e, bv=bv, ci=ci, Lc=Lc: e.copy(vtok[:Lc, ci, :], ps[:Lc, bv, :512]), reads=[PS(bv)], writes=[("vtok", ci)])

        dump("uT", uT[:], [128, 4, 512], BF16, [("uT", m) for m in range(4)], ti, l)
        dump("qk", qk[:], [64, 8, 512], BF16, [("qk", m) for m in range(8)], ti, l)
        dump("kst", kst[:], [64, 8, 256], BF16, [("kst", m) for m in range(8)], ti, l)
        dump("vtok", vtok[:], [64, 8, 512], BF16, [("vtok", m) for m in range(8)], ti, l)
        dump("srT", srT[:], [128, 4, 512], BF16, [("srT", m) for m in range(4)], ti, l)
        dump("dec", dec[:], [64, 8, 4], F32, [("dec", m) for m in range(8)], ti, l)
        P.tag = "gla_core"
        Lc0 = chunks[0][1]
        for h in range(4):
            x = h % 2
            kvb = []
            for g0_ in range(0, nch, 4):
                bkv = P.bank()
                kvb.append(bkv)
                gl = min(4, nch - g0_)
                for ci in range(g0_, g0_ + gl):
                    c0, Lc = chunks[ci]
                    mm(ps[:64, bkv, 128 * (ci - g0_):128 * (ci - g0_) + 128], kst[:Lc, ci, 64 * h:64 * h + 64],
                       vtok[:Lc, ci, 128 * h:128 * h + 128], True, True, [("kst", ci), ("vtok", ci)], [PS(bkv)])
            P.op("dve", lambda e, x=x, h=h: e.tensor_copy(Sall[x][:, 0, :], Scar[:, l, h, :]), reads=[("Scar", l, h)], writes=[("Sall", x)])
            for ci in range(nch):
                bkv = kvb[ci // 4]
                P.op("dve", lambda e, x=x, ci=ci, h=h, bkv=bkv: e.scalar_tensor_tensor(
                    out=Sall[x][:, ci + 1, :], in0=Sall[x][:, ci, :], scalar=dec[:, ci, h:h + 1],
                    in1=ps[:64, bkv, 128 * (ci % 4):128 * (ci % 4) + 128], op0=ALU.mult, op1=ALU.add),
                     reads=[("Sall", x), ("dec", ci), PS(bkv)], writes=[("Sall", x)])
            P.op("dve", lambda e, x=x, h=h: e.tensor_copy(Scar[:, l, h, :], Sall[x][:, nch, :]), reads=[("Sall", x)], writes=[("Scar", l, h)])
            P.op("act", lambda e, x=x, h=h: e.copy(Sbf[h][:, :nch, :], Sall[x][:, :nch, :]), reads=[("Sall", x)], writes=[("Sbf", h)])
        for h in range(4):
            bs = P.bank()
            for ci, (c0, Lc) in enumerate(chunks):
                mm(ps[:Lc, bs, 64 * ci:64 * ci + Lc], qk[:, 4 + h, c0:c0 + Lc], qk[:, h, c0:c0 + Lc], True, True,
                   [("qk", 4 + h), ("qk", h)], [PS(bs)])
            pv = ps[:Lc0, bs, 0:64 * nch].rearrange("p (c t) -> p c t", t=64)[:, :, :Lc0]
            cm = AP(causal, 0, [[64, Lc0], [0, nch], [1, Lc0]])
            P.op("dve", lambda e, h=h, pv=pv, cm=cm: e.tensor_tensor(out=scm[h][:Lc0, :nch, :Lc0], in0=pv, in1=cm, op=ALU.mult),
                 reads=[PS(bs), "causal"], writes=[("scm", h)])
        P.tag = "inproj_r"
        w, wk = W(ti, l, 2)
        wv = w.rearrange("p (k c) -> p k c", k=8)
        for m in range(4):
            b = P.bank()
            for kt in range(8):
                mm(ps[:, b, :N], wv[:, kt, 128 * m:128 * m + 128], zT[:, kt, :N], kt == 0, kt == 7, [wk, zkeys[kt]], [PS(b)])
            P.op("act", lambda e, b=b, m=m: e.activation(out=srT[:, m, :N], in_=ps[:, b, :N], func=AF.Silu),
                 reads=[PS(b)], writes=[("srT", m)])
        P.tag = "gla_core"
        for h in range(4):
            bo = P.bank()
            for ci, (c0, Lc) in enumerate(chunks):
                mm(ps[:, bo, c0:c0 + Lc], vtok[:Lc, ci, 128 * h:128 * h + 128], scm[h][:Lc, ci, :Lc], True, False,
                   [("vtok", ci), ("scm", h)], [PS(bo)])
                mm(ps[:, bo, c0:c0 + Lc], Sbf[h][:, ci, :], qk[:, h, c0:c0 + Lc], False, True, [("Sbf", h), ("qk", h)], [PS(bo)])
            P.op("act", lambda e, bo=bo: e.activation(out=hid[:, 8, :N], in_=ps[:, bo, :N], func=AF.Square), reads=[PS(bo)], writes=[("hid", 8)])
            bm = P.bank()
            mm(ps[:, bm, :N], onesH[:], hid[:, 8, :N], True, True, [("hid", 8), "onesH"], [PS(bm)])
            P.op("act", lambda e, bm=bm: e.activation(out=rstd[:, :N], in_=ps[:, bm, :N], func=AF.Ln, bias=cst[:, 0:1]),
                 reads=[PS(bm), "cst"], writes=["rstd"])
            P.op("act", lambda e: e.activation(out=rstd[:, :N], in_=rstd[:, :N], func=AF.Exp, scale=-0.5), reads=["rstd"], writes=["rstd"])
            P.op("dve", lambda e, bo=bo, h=h: e.scalar_tensor_tensor(out=ftmp[0][:, :N], in0=ps[:, bo, :N], scalar=smalls[:, l, 24 + h:25 + h],
                                                                     in1=rstd[:, :N], op0=ALU.mult, op1=ALU.mult),
                 reads=[PS(bo), "rstd", "smalls"], writes=["ftmp0"])
            P.op("dve", lambda e, h=h: e.tensor_tensor(out=onT[:, h, :N], in0=ftmp[0][:, :N], in1=srT[:, h, :N], op=ALU.mult),
                 reads=["ftmp0", ("srT", h)], writes=[("onT", h)])

        dump("onT", onT[:], [128, 4, 512], BF16, [("onT", m) for m in range(4)], ti, l)
        P.tag = "yb"
        wpb, kpb = W(ti, l, 5)
        wpbv = wpb.rearrange("p (k c) -> p k c", k=4)
        wg = [W(ti, l, 6, keep=1), W(ti, l, 7, keep=2)]
        for m in range(8):
            wgv = wg[m // 4][0].rearrange("p (k c) -> p k c", k=8)
            wgk = wg[m // 4][1]
            bg = P.bank()
            for kt in range(8):
                mm(ps[:, bg, :N], wgv[:, kt, 128 * (m % 4):128 * (m % 4) + 128], zT[:, kt, :N], kt == 0, kt == 7, [wgk, zkeys[kt]], [PS(bg)])
            P.op("act", lambda e, bg=bg: e.activation(out=ftmp[1][:, :N], in_=ps[:, bg, :N], func=AF.Sigmoid), reads=[PS(bg)], writes=["ftmp1"])
            by = P.bank()
            for kt in range(4):
                mm(ps[:, by, :N], wpbv[:, kt, 128 * m:128 * m + 128], onT[:, kt, :N], kt == 0, kt == 3, [kpb, ("onT", kt)], [PS(by)])
            P.op("dve", lambda e, by=by, m=m: e.tensor_tensor(out=mixT[:, m, :N], in0=ps[:, by, :N], in1=ftmp[1][:, :N], op=ALU.mult),
                 reads=[PS(by), "ftmp1"], writes=[("mixT", m)])

        dump("mixb", mixT[:], [128, 8, 512], BF16, [("mixT", m) for m in range(8)], ti, l)
        dump("Gs", Gs[:], [128, 65, 2, 16], F32, ["Gs"], ti, l)
        P.tag = "s5_D"
        KTw, KTk = W(ti, l, 10)
        KTv = KTw.rearrange("p (c i s) -> p c i s", c=4, i=8)
        CA = [W(ti, l, 11, keep=1), W(ti, l, 12, keep=2)]
        for c in range(4):
            for j in range(R):
                by = P.bank()
                for i in range(j + 1):
                    rhs = uT[:, c, :N].rearrange("p (n s) -> p s n", s=R)[:, i, :]
                    mm(ps[:, by, :nb], KTv[:, c, j - i, :], rhs, i == 0, False, [KTk, ("uT", c)], [PS(by)])
                for part in range(2):
                    cav = CA[part][0].rearrange("p (q j s) -> p q j s", q=16, j=8)
                    for r in range(4):
                        kw = {"tile_position": (0, 96)} if r == 3 else {}
                        mm(ps[32 * r:32 * r + 32, by, :nb], cav[:, 4 * c + r, j, :], Gbf[:, part, 4 * c + r, :nb], False,
                           (r == 3 and part == 1), [CA[part][1]] + GBK, [PS(by)], **kw)
                uo = uT[:, c, :N].rearrange("p (n s) -> p s n", s=R)[:, j, :]
                yo = yT[:, c, :N].rearrange("p (n s) -> p s n", s=R)[:, j, :]
                P.op("dve", lambda e, by=by, uo=uo, yo=yo, c=c: e.scalar_tensor_tensor(out=yo, in0=uo, scalar=smalls[:, l, 16 + c:17 + c],
                                                                                       in1=ps[:, by, :nb], op0=ALU.mult, op1=ALU.add),
                     reads=[PS(by), ("uT", c), "smalls"], writes=[("yT", c)])
        for c in range(4):
            P.op("act", lambda e, c=c: e.activation(out=yT[:, c, :N], in_=yT[:, c, :N], func=AF.Gelu_apprx_tanh),
                 reads=[("yT", c)], writes=[("yT", c)])
            P.op("pool", lambda e, c=c: e.tensor_copy(hid[:, 9 + c, :N], yT[:, c, :N]), reads=[("yT", c)], writes=[("hid", 9 + c)])
        dump("act", yT[:], [128, 4, 512], F32, [("yT", m) for m in range(4)], ti, l)
        P.tag = "glu"
        wgl, kgl = W(ti, l, 13)
        wglv = wgl[:, 0:2048].rearrange("p (k c) -> p k c", k=4)
        for m in range(4):
            b = P.bank()
            for kt in range(4):
                mm(ps[:, b, :N], wglv[:, kt, 128 * m:128 * m + 128], hid[:, 9 + kt, :N], kt == 0, kt == 3, [kgl, ("hid", 9 + kt)], [PS(b)])
            P.op("act", lambda e, b=b, m=m: e.activation(out=ftmp[1][:, :N], in_=ps[:, b, :N], func=AF.Sigmoid,
                                                         bias=smalls[:, l, 20 + m:21 + m]), reads=[PS(b), "smalls"], writes=["ftmp1"])
            P.op("dve", lambda e, m=m: e.tensor_tensor(out=hid[:, 13 + m, :N], in0=yT[:, m, :N], in1=ftmp[1][:, :N], op=ALU.mult),
                 reads=[("yT", m), "ftmp1"], writes=[("hid", 13 + m)])
        dump("s5o", hid[:, 13:17, :], [128, 4, 512], BF16, [("hid", 13 + m) for m in range(4)], ti, l)
        P.tag = "ya"
        wpa, kpa = W(ti, l, 14)
        wpav = wpa.rearrange("p (k c) -> p k c", k=4)
        wg = [W(ti, l, 15, keep=1), W(ti, l, 16, keep=2)]
        for m in range(8):
            wgv = wg[m // 4][0].rearrange("p (k c) -> p k c", k=8)
            wgk = wg[m // 4][1]
            bg = P.bank()
            for kt in range(8):
                mm(ps[:, bg, :N], wgv[:, kt, 128 * (m % 4):128 * (m % 4) + 128], zT[:, kt, :N], kt == 0, kt == 7, [wgk, zkeys[kt]], [PS(bg)])
            P.op("act", lambda e, bg=bg: e.activation(out=ftmp[1][:, :N], in_=ps[:, bg, :N], func=AF.Sigmoid), reads=[PS(bg)], writes=["ftmp1"])
            by = P.bank()
            for kt in range(4):
                mm(ps[:, by, :N], wpav[:, kt, 128 * m:128 * m + 128], hid[:, 13 + kt, :N], kt == 0, kt == 3, [kpa, ("hid", 13 + kt)], [PS(by)])
            P.op("dve", lambda e, by=by: e.tensor_tensor(out=ftmp[0][:, :N], in0=ps[:, by, :N], in1=ftmp[1][:, :N], op=ALU.mult),
                 reads=[PS(by), "ftmp1"], writes=["ftmp0"])
            P.op("dve", lambda e, m=m: e.tensor_tensor(out=mixT[:, m, :N], in0=ftmp[0][:, :N], in1=mixT[:, m, :N], op=ALU.add),
                 reads=["ftmp0", ("mixT", m)], writes=[("mixT", m)])
        dump("mix", mixT[:], [128, 8, 512], BF16, [("mixT", m) for m in range(8)], ti, l)
        P.tag = "outproj"
        wo = [W(ti, l, 17), W(ti, l, 18, keep=1)]
        for m in range(8):
            wov = wo[m // 4][0].rearrange("p (k c) -> p k c", k=8)
            b = P.bank()
            for kt in range(8):
                mm(ps[:, b, :N], wov[:, kt, 128 * (m % 4):128 * (m % 4) + 128], mixT[:, kt, :N], kt == 0, kt == 7,
                   [wo[m // 4][1], ("mixT", kt)], [PS(b)])
            P.op("dve", lambda e, b=b, m=m: e.tensor_tensor(out=hT[:, m, :N], in0=hT[:, m, :N], in1=ps[:, b, :N], op=ALU.add),
                 reads=[PS(b), ("hT", m)], writes=[("hT", m)])
        dump("hmid", hT[:], [128, 8, 512], F32, [("hT", m) for m in range(8)], ti, l)
        P.tag = "ffn13"
        rmsnorm(N, lambda dt: smalls[:, l, 8 + dt:9 + dt], zT, "zT")
        for i in range(11):
            w, wk = W(ti, l, 19 + i)
            wv = w.rearrange("p (a k c) -> p a k c", a=2, k=8)
            for f2 in range(2):
                f = 2 * i + f2
                b1 = P.bank()
                for kt in range(8):
                    mm(ps[:, b1, :N], wv[:, 0, kt, 128 * f2:128 * f2 + 128], zT[:, kt, :N], kt == 0, kt == 7, [wk, zkeys[kt]], [PS(b1)])
                b3 = P.bank()
                for kt in range(8):
                    mm(ps[:, b3, :N], wv[:, 1, kt, 128 * f2:128 * f2 + 128], zT[:, kt, :N], kt == 0, kt == 7, [wk, zkeys[kt]], [PS(b3)])
                x = f % 2
                P.op("act", lambda e, b1=b1, x=x: e.activation(out=ftmp[x][:, :N], in_=ps[:, b1, :N], func=AF.Silu),
                     reads=[PS(b1)], writes=[f"ftmp{x}"])
                P.op("dve", lambda e, b3=b3, x=x, f=f: e.tensor_tensor(out=hid[:, f, :N], in0=ftmp[x][:, :N], in1=ps[:, b3, :N], op=ALU.mult),
                     reads=[PS(b3), f"ftmp{x}"], writes=[("hid", f)])
        P.tag = "ffn2"
        for m in range(8):
            w, wk = W(ti, l, 30 + m)
            wv = w[:, 0:22 * 128].rearrange("p (f c) -> p f c", f=22)
            b = P.bank()
            for f in range(22):
                mm(ps[:, b, :N], wv[:, f, :], hid[:, f, :N], f == 0, f == 21, [wk, ("hid", f)], [PS(b)])
            P.op("dve", lambda e, b=b, m=m: e.tensor_tensor(out=hT[:, m, :N], in0=hT[:, m, :N], in1=ps[:, b, :N], op=ALU.add),
                 reads=[PS(b), ("hT", m)], writes=[("hT", m)])
        dump("hend", hT[:], [128, 8, 512], F32, [("hT", m) for m in range(8)], ti, l)
        P.tag = "final"
        if l == L - 1 and ti > 0:
            rmsnorm(N, lambda dt: gf[:, dt:dt + 1], None, "yout", final_store=(t0 - NM))

    for ti in range(len(tiles)):
        for l in range(L):
            tile_layer(ti, l)
    P.barrier()
    P.emit()
    nc._prog = P
    return nc


_TILES_FULL = [(0, 16)] + [(16 + 512 * i, 512) for i in range(8)]


def make_in_map(inp, b, L, T):
    xT = np.empty((D, T), np.float32)
    xT[:, :NM] = inp["meta"].T
    xT[:, NM:] = inp["x"][b][:T - NM].T
    return {"xT": xT}


def shared_maps(inp, L):
    wpack = np.stack([pack_layer_weights(inp, l) for l in range(L)])
    s5 = [pack_s5(inp, l) for l in range(L)]
    sm, gf, walp = pack_small(inp, L)
    return {"wpack": wpack, "s5lam": np.stack([s[0] for s in s5]), "s5b": np.stack([s[1] for s in s5]),
            "s5c": np.stack([s[2] for s in s5]), "small": sm, "gfin": gf, "walp": walp}


def kernel(**inputs):
    inp = {k: np.asarray(v, dtype=np.float32) for k, v in inputs.items()}
    L = inp["w_in"].shape[0]
    B, SEQ, _ = inp["x"].shape
    T = SEQ + NM
    tiles = [(0, 16)] + [(16 + 512 * i, 512) for i in range(SEQ // 512)]
    import time as _t
    _t0 = _t.time()
    nc = build_nc(L, tiles)
    print("build_nc s", _t.time() - _t0, flush=True)
    shared = shared_maps(inp, L)
    in_maps = []
    for b in range(B):
        m = dict(shared)
        m.update(make_in_map(inp, b, L, T))
        in_maps.append(m)
    _t0 = _t.time()
    res = run_bass_kernel_spmd(nc, in_maps, core_ids=list(range(B)))
    print("run s", _t.time() - _t0, flush=True)
    out = np.empty((B, SEQ, D), np.float32)
    for b in range(B):
        out[b] = res.results[b]["outT"].T
    if DBG["on"]:
        DBG["res"] = {k: np.asarray(res.results[0][k]) for k in DBG["names"]}
    return out
```

```python
import numpy as np
from contextlib import ExitStack
import concourse.bass as bass
import concourse.mybir as mybir
from concourse.bass_utils import run_bass_kernel_spmd

F32 = mybir.dt.float32
BF16 = mybir.dt.bfloat16
AF = mybir.ActivationFunctionType
ALU = mybir.AluOpType
AP = bass.AP

D = 1024
NM = 16
R = 8
EPS = 1e-6
NCH = 38
CHW = 4096
DERIVED = (8, 9, 10, 11, 12)
HOST_CH = [c for c in range(NCH) if c not in DERIVED]
NS = 6
CH = 8000
NDMASEM = 8
ENGS = ("pe", "dve", "act", "pool", "sp")
SYNC_SAME = {"dve", "act"}


class Prog:
    def __init__(self, nc):
        self.nc = nc
        self.ops = {e: [] for e in ENGS}
        self.nops = {e: 0 for e in ENGS}
        self.signaled = {e: set() for e in ENGS}
        self.sems = {}
        self.seen = {e: {} for e in ENGS}
        self.lastw = {}
        self.readers = {}
        self.dma_n = {e: 0 for e in ENGS}
        self.nbank = 0
        self.held = set()
        self.tag = "init"
        self.tags = {e: [] for e in ENGS}

    def bank(self, hold=False):
        while True:
            b = self.nbank % 8
            self.nbank += 1
            if b not in self.held:
                break
        if hold:
            self.held.add(b)
        return b

    def release(self, b):
        self.held.discard(b)

    def _sem(self, prod, idx):
        k = (prod, idx)
        if k not in self.sems:
            self.sems[k] = self.nc.alloc_semaphore(name=f"s_{prod}_{idx}")
        return self.sems[k]

    def _ref(self, eng, p, t, waits):
        if self.seen[eng].get(p, 0) >= t:
            return
        self.seen[eng][p] = t
        if not p.startswith("dma:"):
            self.signaled[p].add(t)
        waits.append((p, t))

    def _deps(self, eng, reads, writes, force_same=False):
        need = {}

        def add(pt):
            if pt is None:
                return
            p, t = pt
            if need.get(p, 0) < t:
                need[p] = t
        for k in reads:
            add(self.lastw.get(k))
        for k in writes:
            add(self.lastw.get(k))
            for pt in self.readers.get(k, ()):
                add(pt)
        waits = []
        for p, t in need.items():
            if p == eng and not (eng in SYNC_SAME or force_same):
                continue
            self._ref(eng, p, t, waits)
        return waits

    def _commit(self, me, reads, writes):
        for k in reads:
            lst = self.readers.setdefault(k, [])
            lst[:] = [pt for pt in lst if pt[0] != me[0]]
            lst.append(me)
        for k in writes:
            self.lastw[k] = me
            self.readers[k] = []

    def op(self, eng, fn, reads=(), writes=()):
        waits = self._deps(eng, reads, writes)
        self.nops[eng] += 1
        t = self.nops[eng]
        self.ops[eng].append((waits, fn, ("op", t)))
        self.tags[eng].append(self.tag)
        self._commit((eng, t), reads, writes)

    def dma(self, eng, fn, reads=(), writes=()):
        n = self.dma_n[eng]
        self.dma_n[eng] += 1
        slot = n % NDMASEM
        gen = n // NDMASEM + 1
        prod = f"dma:{eng}/{slot}"
        waits = self._deps(eng, reads, writes, force_same=True)
        if gen > 1:
            self._ref(eng, prod, gen - 1, waits)
        self.ops[eng].append((waits, fn, ("dma", self._sem("dma_" + eng, slot))))
        self._commit((prod, gen), reads, writes)

    def barrier(self):
        last = {}
        for e in ENGS:
            if self.nops[e]:
                last[e] = self.nops[e]
            n = self.dma_n[e]
            for slot in range(min(n, NDMASEM)):
                last[f"dma:{e}/{slot}"] = (n - 1 - slot) // NDMASEM + 1
        for e in ENGS:
            waits = []
            for p, t in last.items():
                if p == e:
                    continue
                self._ref(e, p, t, waits)
            if waits:
                self.ops[e].append((waits, None, None))

    def emit(self):
        engmap = {"pe": "tensor", "dve": "vector", "act": "scalar", "pool": "gpsimd", "sp": "sync"}
        ticket = {}
        for e in ENGS:
            for k, idx in enumerate(sorted(self.signaled[e])):
                ticket[(e, idx)] = k + 1

        def semval(p, t):
            if p.startswith("dma:"):
                q, slot = p[4:].split("/")
                return (self._sem("dma_" + q, int(slot)), 16 * t)
            tk = ticket[(p, t)]
            return (self._sem(p, (tk - 1) // CH), (tk - 1) % CH + 1)
        with self.nc.Block() as block:
            for e in ENGS:
                ops = self.ops[e]

                def body(engobj, ops=ops, e=e):
                    for waits, fn, inc in ops:
                        for (p, t) in waits:
                            sv = semval(p, t)
                            engobj.wait_ge(sv[0], sv[1])
                        if fn is None:
                            continue
                        ins = fn(engobj)
                        if inc[0] == "dma":
                            ins.then_inc(inc[1], 16)
                        elif (e, inc[1]) in ticket:
                            tk = ticket[(e, inc[1])]
                            ins.then_inc(self._sem(e, (tk - 1) // CH), 1)
                getattr(block, engmap[e])(body)


def _lhsT_img(W, col0, ncols):
    K = W.shape[0]
    return np.ascontiguousarray(W[:, col0:col0 + ncols].reshape(K // 128, 128, ncols).transpose(1, 0, 2))


def pack_layer_weights(inp, l):
    out = np.zeros((len(HOST_CH), 128, CHW), np.float32)
    w_in = inp["w_in"][l]

    def put(c, img, off=0):
        flat = img.reshape(128, -1)
        out[HOST_CH.index(c), :, off:off + flat.shape[1]] = flat
    put(0, _lhsT_img(w_in, 0, 512))
    put(1, _lhsT_img(w_in, 512, 512))
    put(2, _lhsT_img(w_in, 1536, 512))
    put(3, _lhsT_img(w_in, 768, 256))
    put(3, _lhsT_img(w_in, 2048, 16), 2048)
    put(4, _lhsT_img(w_in, 1024, 512))
    put(5, _lhsT_img(inp["w_pb"][l], 0, 1024))
    put(6, _lhsT_img(w_in, 3088, 512))
    put(7, _lhsT_img(w_in, 3600, 512))
    put(13, _lhsT_img(inp["w_glu"][l], 0, 512))
    put(14, _lhsT_img(inp["w_pa"][l], 0, 1024))
    put(15, _lhsT_img(w_in, 2064, 512))
    put(16, _lhsT_img(w_in, 2576, 512))
    put(17, _lhsT_img(inp["w_out"][l], 0, 512))
    put(18, _lhsT_img(inp["w_out"][l], 512, 512))
    w1, w3, w2 = inp["w_ff1"][l], inp["w_ff3"][l], inp["w_ff2"][l]
    for i in range(11):
        img = np.stack([_lhsT_img(w1, 256 * i, 256), _lhsT_img(w3, 256 * i, 256)], axis=1)
        put(19 + i, img)
    for m in range(8):
        put(30 + m, _lhsT_img(w2, 128 * m, 128))
    return out


def pack_s5(inp, l):
    lam = np.zeros((128, 3, 16), np.float32)
    bS = np.zeros((128, 2, 16, 32), np.float32)
    cS = np.zeros((128, 2, 16, 32), np.float32)
    for q in range(16):
        for gg in range(2):
            g = 2 * q + gg
            sl = slice(gg * 64, gg * 64 + 64)
            lam[sl, 0, q] = inp["lam_re"][l, g]
            lam[sl, 1, q] = inp["lam_im"][l, g]
            lam[sl, 2, q] = inp["log_step"][l, g]
            bS[sl, 0, q, gg * 16:gg * 16 + 16] = inp["b_re"][l, g]
            bS[sl, 1, q, gg * 16:gg * 16 + 16] = inp["b_im"][l, g]
            cS[sl, 0, q, gg * 16:gg * 16 + 16] = inp["c_re"][l, g].T
            cS[sl, 1, q, gg * 16:gg * 16 + 16] = inp["c_im"][l, g].T
    return lam, bS, cS


def pack_small(inp, L):
    sm = np.zeros((128, L, 28), np.float32)
    for l in range(L):
        sm[:, l, 0:8] = inp["norm1"][l].reshape(8, 128).T
        sm[:, l, 8:16] = inp["norm2"][l].reshape(8, 128).T
        sm[:, l, 16:20] = inp["d_skip"][l].reshape(4, 128).T
        sm[:, l, 20:24] = inp["b_glu"][l].reshape(4, 128).T
        sm[:, l, 24:28] = inp["gla_norm"][l].reshape(4, 128).T
    gf = np.ascontiguousarray(inp["norm_f"].reshape(8, 128).T)
    walp = np.zeros((32, L, 256), np.float32)
    for l in range(L):
        walp[0:16, l] = inp["w_alpha"][l]
        walp[16, l] = inp["b_alpha"][l]
    return sm, gf, walp


DBG = {"on": False, "names": []}


def build_nc(L, tiles):
    T = tiles[-1][0] + tiles[-1][1]
    TOUT = T - NM
    nc = bass.Bass("TRN2", target_bir_lowering=False)
    P = Prog(nc)
    din = lambda n, s: nc.dram_tensor(n, s, F32, kind="ExternalInput").ap()
    xT = din("xT", [D, T])
    wpack = din("wpack", [L, len(HOST_CH), 128, CHW])
    s5lam = din("s5lam", [L, 128, 3, 16])
    s5b = din("s5b", [L, 128, 2, 16, 32])
    s5c = din("s5c", [L, 128, 2, 16, 32])
    small = din("small", [128, L, 28])
    gfin = din("gfin", [128, 8])
    walp_d = din("walp", [32, L, 256])
    outT = nc.dram_tensor("outT", [D, TOUT], F32, kind="ExternalOutput").ap()
    wbf = nc.dram_tensor("wbf", [L, NCH, 128, CHW], BF16, kind="Internal").ap()

    sb = lambda n, s, d=F32: nc.alloc_sbuf_tensor(n, s, d)
    ps = nc.alloc_psum_tensor("ps", [128, 8, 512], F32)
    PS = lambda b: ("ps", b)

    wring = sb("wring", [128, NS, CHW], BF16)
    ident = sb("ident", [128, 128])
    onesD = sb("onesD", [128, 128], BF16)
    onesH = sb("onesH", [128, 128], BF16)
    maskU = sb("maskU", [64, 64], BF16)
    maskL = sb("maskL", [64, 64], BF16)
    causal = sb("causal", [64, 64])
    cst = sb("cst", [128, 4])
    smalls = sb("smalls", [128, L, 28])
    gf = sb("gf", [128, 8])
    walp = sb("walp_s", [32, L, 256])
    A1 = sb("A1", [128, L, 2, 16])
    A2 = sb("A2", [128, L, 2, 16])
    Gcar = sb("Gcar", [128, L, 2, 16])
    Scar = sb("Scar", [64, L, 4, 128])
    tmpc = sb("tmpc", [128, 128])

    def v(e_, f):
        return f

    P.op("pool", lambda e: e.memset(tmpc[:], 1.0), writes=["tmpc"])
    P.op("pool", lambda e: e.affine_select(out=ident[:], in_=tmpc[:], pattern=[[1, 128]], compare_op=ALU.is_equal,
                                           fill=0.0, base=0, channel_multiplier=-1), reads=["tmpc"], writes=["ident"])
    P.op("pool", lambda e: e.memset(onesD[:], 1.0 / 1024), writes=["onesD"])
    P.op("pool", lambda e: e.memset(onesH[:], 1.0 / 128), writes=["onesH"])
    P.op("pool", lambda e: e.memset(cst[:, 0:1], EPS), writes=["cst"])
    P.op("pool", lambda e: e.memset(cst[:, 1:2], 1.0), writes=["cst"])
    P.op("pool", lambda e: e.memset(cst[:, 2:3], 0.0), writes=["cst"])
    P.op("pool", lambda e: e.affine_select(out=causal[:], in_=tmpc[0:64, 0:64], pattern=[[1, 64]], compare_op=ALU.is_ge,
                                           fill=0.0, base=0, channel_multiplier=-1), reads=["tmpc"], writes=["causal"])
    P.op("pool", lambda e: e.tensor_scalar(out=maskU[:], in0=causal[:], scalar1=-1.0 / 16, scalar2=None, op0=ALU.mult),
         reads=["causal"], writes=["maskU"])
    P.op("pool", lambda e: e.tensor_scalar(out=maskL[:], in0=causal[:], scalar1=1.0 / 16, scalar2=-1.0 / 16, op0=ALU.mult,
                                           op1=ALU.add), reads=["causal"], writes=["maskL"])
    P.dma("sp", lambda e: e.dma_start(out=smalls[:], in_=small[:, :, :]), writes=["smalls"])
    P.dma("sp", lambda e: e.dma_start(out=gf[:], in_=gfin[:, :]), writes=["gf"])
    P.dma("sp", lambda e: e.dma_start(out=walp[:], in_=walp_d[:, :, :]), writes=["walp"])
    P.op("pool", lambda e: e.memset(Gcar[:], 0.0), writes=["Gcar"])
    P.op("pool", lambda e: e.memset(Scar[:], 0.0), writes=["Scar"])

    for l in range(L):
        for ci, c in enumerate(HOST_CH):
            P.dma("pool", lambda e, l=l, ci=ci, c=c: e.dma_start(out=wbf[l, c], in_=wpack[l, ci]),
                  writes=[("wbf", l, c)])

    with ExitStack() as es:
        tb = lambda nme, s, d=F32: es.enter_context(nc.sbuf_tensor(nme, s, d))
        lam = tb("p_lam", [128, 3, 16])
        bS = tb("p_bS", [128, 2, 16, 32])
        cS = tb("p_cS", [128, 2, 16, 32])
        ncSi = tb("p_ncSi", [128, 16, 32])
        sc = tb("p_sc", [128, 32, 16])
        Mr = [tb(f"p_Mr{i}", [128, 16, 32]) for i in range(2)]
        Mi = [tb(f"p_Mi{i}", [128, 16, 32]) for i in range(2)]
        t32 = [tb(f"p_t32{i}", [128, 16, 32]) for i in range(2)]
        BD = tb("p_BD", [128, 128])
        stg = {c: tb(f"p_stg{c}", [128, CHW], BF16) for c in DERIVED}
        P.op("dve", lambda e: e.memset(BD[:], 0.0), writes=["BD"])
        for r in range(4):
            P.op("dve", lambda e, r=r: e.memset(BD[32 * r:32 * r + 32, 32 * r:32 * r + 32], 1.0), reads=["BD"], writes=["BD"])
        WBv = {0: stg[8][:].rearrange("p (c i s) -> p c i s", c=4, i=8), 1: stg[9][:].rearrange("p (c i s) -> p c i s", c=4, i=8)}
        KTv = stg[10][:].rearrange("p (c i s) -> p c i s", c=4, i=8)
        CAv = {0: stg[11][:].rearrange("p (q j s) -> p q j s", q=16, j=8), 1: stg[12][:].rearrange("p (q j s) -> p q j s", q=16, j=8)}
        S = lambda i: sc[:, i, :]
        K = lambda i: ("sc", i)

        def bc(a16):
            return AP(a16.tensor, a16.offset, [list(a16.ap[0]), list(a16.ap[1]), [0, 32]])

        def tt(out, a, b, op, r, w, eng="dve"):
            P.op(eng, lambda e: e.tensor_tensor(out=out, in0=a, in1=b, op=op), reads=r, writes=w)

        def ts(out, a, s1, s2, op0, op1, r, w, eng="dve"):
            if op1 is None:
                P.op(eng, lambda e: e.tensor_scalar(out=out, in0=a, scalar1=s1, scalar2=None, op0=op0), reads=r, writes=w)
            else:
                P.op(eng, lambda e: e.tensor_scalar(out=out, in0=a, scalar1=s1, scalar2=s2, op0=op0, op1=op1), reads=r, writes=w)

        def act(out, a, fn, r, w, **kw):
            P.op("act", lambda e: e.activation(out=out, in_=a, func=fn, **kw), reads=r, writes=w)

        TWO_PI = 2.0 * np.pi
        C1 = 6.28125
        C2 = TWO_PI - C1
        MAGIC = 12582912.0

        def sin_of(dst, src_i, shift, ktmp):
            y, t1, t2 = ktmp
            ts(S(y), S(src_i), shift, None, ALU.add, None, [K(src_i)], [K(y)])
            ts(S(t1), S(y), 1.0 / TWO_PI, None, ALU.mult, None, [K(y)], [K(t1)])
            ts(S(t2), S(t1), MAGIC, None, ALU.add, None, [K(t1)], [K(t2)])
            ts(S(t1), S(t2), -MAGIC, None, ALU.add, None, [K(t2)], [K(t1)])
            P.op("dve", lambda e: e.scalar_tensor_tensor(out=S(t2), in0=S(t1), scalar=-C1, in1=S(y), op0=ALU.mult, op1=ALU.add),
                 reads=[K(t1), K(y)], writes=[K(t2)])
            P.op("dve", lambda e: e.scalar_tensor_tensor(out=S(y), in0=S(t1), scalar=-C2, in1=S(t2), op0=ALU.mult, op1=ALU.add),
                 reads=[K(t1), K(t2)], writes=[K(y)])
            ts(S(y), S(y), float(np.pi), float(-np.pi), ALU.min, ALU.max, [K(y)], [K(y)])
            act(S(dst), S(y), AF.Sin, [K(y)], [K(dst)])

        for l in range(L):
            P.dma("sp", lambda e, l=l: e.dma_start(out=lam[:], in_=s5lam[l]), writes=["lam"])
            P.dma("sp", lambda e, l=l: e.dma_start(out=bS[:], in_=s5b[l]), writes=["bS"])
            P.dma("sp", lambda e, l=l: e.dma_start(out=cS[:], in_=s5c[l]), writes=["cS"])
            ts(ncSi[:], cS[:, 1], -1.0, None, ALU.mult, None, ["cS"], ["ncSi"])
            STEP, LR, LI, MAG, ANG, COS, SIN, ARE, AIM, DEN, NR, CFR, CFI, T0, T1, T2, T3, PR, PI_, PR2, PI2 = range(21)
            act(S(STEP), lam[:, 2, :], AF.Exp, ["lam"], [K(STEP)])
            ts(S(LR), lam[:, 0, :], -1e-4, None, ALU.min, None, ["lam"], [K(LR)])
            ts(S(LI), lam[:, 1, :], 1.0, None, ALU.mult, None, ["lam"], [K(LI)])
            tt(S(T0), S(LR), S(STEP), ALU.mult, [K(LR), K(STEP)], [K(T0)])
            act(S(MAG), S(T0), AF.Exp, [K(T0)], [K(MAG)])
            tt(S(ANG), S(LI), S(STEP), ALU.mult, [K(LI), K(STEP)], [K(ANG)])
            sin_of(SIN, ANG, 0.0, (T1, T2, T3))
            sin_of(COS, ANG, float(np.pi / 2), (T1, T2, T3))
            tt(S(ARE), S(MAG), S(COS), ALU.mult, [K(MAG), K(COS)], [K(ARE)])
            tt(S(AIM), S(MAG), S(SIN), ALU.mult, [K(MAG), K(SIN)], [K(AIM)])
            tt(S(T0), S(LR), S(LR), ALU.mult, [K(LR)], [K(T0)])
            tt(S(T1), S(LI), S(LI), ALU.mult, [K(LI)], [K(T1)])
            tt(S(DEN), S(T0), S(T1), ALU.add, [K(T0), K(T1)], [K(DEN)])
            P.op("dve", lambda e: e.reciprocal(S(DEN), S(DEN)), reads=[K(DEN)], writes=[K(DEN)])
            ts(S(NR), S(ARE), -1.0, None, ALU.add, None, [K(ARE)], [K(NR)])
            tt(S(T0), S(NR), S(LR), ALU.mult, [K(NR), K(LR)], [K(T0)])
            tt(S(T1), S(AIM), S(LI), ALU.mult, [K(AIM), K(LI)], [K(T1)])
            tt(S(T0), S(T0), S(T1), ALU.add, [K(T0), K(T1)], [K(T0)])
            tt(S(CFR), S(T0), S(DEN), ALU.mult, [K(T0), K(DEN)], [K(CFR)])
            tt(S(T0), S(AIM), S(LR), ALU.mult, [K(AIM), K(LR)], [K(T0)])
            tt(S(T1), S(NR), S(LI), ALU.mult, [K(NR), K(LI)], [K(T1)])
            tt(S(T0), S(T0), S(T1), ALU.subtract, [K(T0), K(T1)], [K(T0)])
            tt(S(CFI), S(T0), S(DEN), ALU.mult, [K(T0), K(DEN)], [K(CFI)])

            def cmul(dr, di, ar_i, ai_i, xr, xi, rk, wk):
                tt(t32[0][:], xr, bc(S(ar_i)), ALU.mult, rk + [K(ar_i)], ["t32a"])
                tt(t32[1][:], xi, bc(S(ai_i)), ALU.mult, rk + [K(ai_i)], ["t32b"])
                tt(dr, t32[0][:], t32[1][:], ALU.subtract, ["t32a", "t32b"], [wk[0]])
                tt(t32[0][:], xi, bc(S(ar_i)), ALU.mult, rk + [K(ar_i)], ["t32a"])
                tt(t32[1][:], xr, bc(S(ai_i)), ALU.mult, rk + [K(ai_i)], ["t32b"])
                tt(di, t32[0][:], t32[1][:], ALU.add, ["t32a", "t32b"], [wk[1]])

            cur = 0
            cmul(Mr[0][:], Mi[0][:], CFR, CFI, bS[:, 0], bS[:, 1], ["bS"], [("M", 0, 0), ("M", 0, 1)])
            for tau in range(R):
                mk = [("M", cur, 0), ("M", cur, 1)]
                for c in range(4):
                    b = P.bank()
                    mr = Mr[cur][:, 4 * c:4 * c + 4, :].rearrange("p a b -> p (a b)")
                    mi = Mi[cur][:, 4 * c:4 * c + 4, :].rearrange("p a b -> p (a b)")
                    cr = cS[:, 0, 4 * c:4 * c + 4, :].rearrange("p a b -> p (a b)")
                    nci = ncSi[:, 4 * c:4 * c + 4, :].rearrange("p a b -> p (a b)")
                    P.op("pe", lambda e, b=b, mr=mr, cr=cr: e.matmul(ps[:, b, 0:128], mr, cr, start=True, stop=False),
                         reads=[mk[0], "cS"], writes=[PS(b)])
                    P.op("pe", lambda e, b=b, mi=mi, nci=nci: e.matmul(ps[:, b, 0:128], mi, nci, start=False, stop=True),
                         reads=[mk[1], "ncSi"], writes=[PS(b)])
                    P.op("dve", lambda e, b=b, c=c, tau=tau: e.tensor_tensor(out=KTv[:, c, tau, :], in0=ps[:, b, 0:128], in1=BD[:],
                                                                               op=ALU.mult), reads=[PS(b), "BD"], writes=[("stg", 10)])
                    for part, Mx in ((0, mr), (1, mi)):
                        b2 = P.bank()
                        P.op("pe", lambda e, b2=b2, Mx=Mx: e.transpose(ps[:, b2, 0:128], Mx, ident[:]),
                             reads=[mk[part], "ident"], writes=[PS(b2)])
                        P.op("act", lambda e, b2=b2, part=part, c=c, tau=tau: e.copy(WBv[part][:, c, R - 1 - tau, :], ps[:, b2, 0:128]),
                             reads=[PS(b2)], writes=[("stg", 8 + part)])
                if tau < R - 1:
                    nxt = 1 - cur
                    cmul(Mr[nxt][:], Mi[nxt][:], ARE, AIM, Mr[cur][:], Mi[cur][:], mk, [("M", nxt, 0), ("M", nxt, 1)])
                    cur = nxt
            ts(S(PR), S(ARE), 1.0, None, ALU.mult, None, [K(ARE)], [K(PR)])
            ts(S(PI_), S(AIM), 1.0, None, ALU.mult, None, [K(AIM)], [K(PI_)])
            pr, pi, pr2, pi2 = PR, PI_, PR2, PI2
            for j in range(R):
                tt(t32[0][:], cS[:, 0], bc(S(pr)), ALU.mult, ["cS", K(pr)], ["t32a"])
                tt(t32[1][:], cS[:, 1], bc(S(pi)), ALU.mult, ["cS", K(pi)], ["t32b"])
                tt(CAv[0][:, :, j, :], t32[0][:], t32[1][:], ALU.subtract, ["t32a", "t32b"], [("stg", 11)])
                tt(t32[0][:], ncSi[:], bc(S(pr)), ALU.mult, ["ncSi", K(pr)], ["t32a"])
                tt(t32[1][:], cS[:, 0], bc(S(pi)), ALU.mult, ["cS", K(pi)], ["t32b"])
                tt(CAv[1][:, :, j, :], t32[0][:], t32[1][:], ALU.subtract, ["t32a", "t32b"], [("stg", 12)])
                if j < R - 1:
                    tt(S(T0), S(pr), S(ARE), ALU.mult, [K(pr), K(ARE)], [K(T0)])
                    tt(S(T1), S(pi), S(AIM), ALU.mult, [K(pi), K(AIM)], [K(T1)])
                    tt(S(pr2), S(T0), S(T1), ALU.subtract, [K(T0), K(T1)], [K(pr2)])
                    tt(S(T0), S(pr), S(AIM), ALU.mult, [K(pr), K(AIM)], [K(T0)])
                    tt(S(T1), S(pi), S(ARE), ALU.mult, [K(pi), K(ARE)], [K(T1)])
                    tt(S(pi2), S(T0), S(T1), ALU.add, [K(T0), K(T1)], [K(pi2)])
                    pr, pi, pr2, pi2 = pr2, pi2, pr, pi
            for h2 in range(2):
                ts(A1[:, l, h2, :], S(pr), 1.0, None, ALU.mult, None, [K(pr)], ["A1"])
            ts(A2[:, l, 0, :], S(pi), -1.0, None, ALU.mult, None, [K(pi)], ["A2"])
            ts(A2[:, l, 1, :], S(pi), 1.0, None, ALU.mult, None, [K(pi)], ["A2"])
            for c in DERIVED:
                P.dma("sp", lambda e, l=l, c=c: e.dma_start(out=wbf[l, c], in_=stg[c][:]), reads=[("stg", c)],
                      writes=[("wbf", l, c)])
        P.barrier()

    NT = 512
    hT = sb("hT", [128, 8, NT])
    zT = sb("zT", [128, 8, NT], BF16)
    hid = sb("hid", [128, 22, NT], BF16)
    rstd = sb("rstd", [128, NT])
    uT = sb("uT", [128, 4, NT], BF16)
    qk = sb("qk", [64, 8, NT], BF16)
    srT = sb("srT", [128, 4, NT], BF16)
    alT = sb("alT", [32, NT])
    mixT = sb("mixT", [128, 8, NT], BF16)
    ftmp = [sb(f"ftmp{i}", [128, NT]) for i in range(2)]
    vtok = sb("vtok", [64, 8, 512], BF16)
    kst = sb("kst", [64, 8, 256], BF16)
    SPt = [sb(f"SPt{i}", [64, 256], BF16) for i in range(2)]
    E3t = [sb(f"E3t{i}", [64, 256]) for i in range(2)]
    etm = [sb(f"etm{i}", [64, 256]) for i in range(2)]
    E12 = [sb(f"E12_{i}", [64, 2, 4, 64]) for i in range(2)]
    dec = sb("dec", [64, 8, 4])
    scm = [sb(f"scm{i}", [64, 8, 64], BF16) for i in range(4)]
    Sall = [sb(f"Sall{i}", [64, 9, 128]) for i in range(2)]
    Sbf = [sb(f"Sbf{i}", [64, 8, 128], BF16) for i in range(4)]
    onT = sb("onT", [128, 4, NT], BF16)
    Gs = sb("Gs", [128, 65, 2, 16])
    Gbf = hid[:, 17:21, :].rearrange("p m n -> p (m n)").rearrange("p (a q b) -> p a q b", a=2, q=16)
    GBK = [("hid", 17 + i) for i in range(4)]
    gt = [sb(f"gt{i}", [128, 2, 16]) for i in range(2)]
    yT = sb("yT", [128, 4, NT])

    P.op("pool", lambda e: e.memset(alT[:], 1.0), writes=["alT"])

    ORDER = [0, 8, 9, 1, 3, 4, 2] + [c for c in range(5, NCH) if c not in (8, 9)]
    sched = [(ti, l, c) for ti in range(len(tiles)) for l in range(L) for c in ORDER]
    wstate = {"issued": 0, "next": 0}

    def issue_loads(upto):
        while wstate["issued"] < min(upto, len(sched)):
            i = wstate["issued"]
            _, l, c = sched[i]
            s = i % NS
            P.dma("sp", lambda e, l=l, c=c, s=s: e.dma_start(out=wring[:, s, :], in_=wbf[l, c]),
                  reads=[("wbf", l, c)], writes=[("w", s)])
            wstate["issued"] += 1

    def W(ti, l, c, keep=0):
        i = wstate["next"]
        assert sched[i] == (ti, l, c), (sched[i], (ti, l, c))
        wstate["next"] += 1
        issue_loads(max(i + 1, i - keep + NS))
        s = i % NS
        return wring[:, s, :], ("w", s)

    def dump(name, tens, shape, dtype, keys, ti, l):
        if not DBG["on"] or not (ti == 1 and l == 0):
            return
        dd = nc.dram_tensor("dbg_" + name, list(shape), dtype, kind="ExternalOutput").ap()
        DBG["names"].append("dbg_" + name)
        P.dma("sp", lambda e: e.dma_start(out=dd, in_=tens), reads=keys, writes=["dbg_" + name])

    def mm(out, lhsT, rhs, start, stop, reads, writes, **kw):
        P.op("pe", lambda e: e.matmul(out, lhsT, rhs, start=start, stop=stop, **kw), reads=reads, writes=writes)

    def rmsnorm(N, gain_col, dst, dkey, final_store=None):
        for dt in range(8):
            P.op("act", lambda e, dt=dt: e.activation(out=hid[:, dt, :N], in_=hT[:, dt, :N], func=AF.Square),
                 reads=[("hT", dt)], writes=[("hid", dt)])
        b = P.bank()
        for dt in range(8):
            mm(ps[:, b, :N], onesD[:], hid[:, dt, :N], dt == 0, dt == 7, [("hid", dt), "onesD"], [PS(b)])
        P.op("act", lambda e: e.activation(out=rstd[:, :N], in_=ps[:, b, :N], func=AF.Ln, bias=cst[:, 0:1]),
             reads=[PS(b), "cst"], writes=["rstd"])
        P.op("act", lambda e: e.activation(out=rstd[:, :N], in_=rstd[:, :N], func=AF.Exp, scale=-0.5),
             reads=["rstd"], writes=["rstd"])
        if final_store is not None:
            for dt in range(8):
                x = dt % 2
                P.op("dve", lambda e, dt=dt, x=x: e.scalar_tensor_tensor(out=ftmp[x][:, :N], in0=hT[:, dt, :N], scalar=gain_col(dt),
                                                                         in1=rstd[:, :N], op0=ALU.mult, op1=ALU.mult),
                     reads=[("hT", dt), "rstd", "smalls", "gf"], writes=[f"ftmp{x}"])
                P.dma("sp", lambda e, dt=dt, x=x: e.dma_start(out=outT[128 * dt:128 * dt + 128, final_store:final_store + N],
                                                              in_=ftmp[x][:, :N]), reads=[f"ftmp{x}"], writes=[("outT", dt)])
            return
        for dt in range(8):
            P.op("dve", lambda e, dt=dt: e.scalar_tensor_tensor(out=dst[:, dt, :N], in0=hT[:, dt, :N], scalar=gain_col(dt),
                                                                 in1=rstd[:, :N], op0=ALU.mult, op1=ALU.mult),
                 reads=[("hT", dt), "rstd", "smalls", "gf"], writes=[(dkey, dt)])

    zkeys = [("zT", kt) for kt in range(8)]

    def tile_layer(ti, l):
        t0, N = tiles[ti]
        nb = N // R
        if N >= 64:
            chunks = [(64 * i, 64) for i in range(N // 64)]
        else:
            chunks = [(0, N)]
        nch = len(chunks)
        sm = lambda a, b_: smalls[:, l, a:b_]

        if l == 0:
            P.dma("sp", lambda e: e.dma_start(out=hT[:, :, :N], in_=xT.rearrange("(dt p) t -> p dt t", p=128)[:, :, t0:t0 + N]),
                  writes=[("hT", dt) for dt in range(8)])
        P.tag = "norm1"
        rmsnorm(N, lambda dt: smalls[:, l, dt:dt + 1], zT, "zT")

        dump("zT", zT[:], [128, 8, 512], BF16, zkeys, ti, l)
        P.tag = "inproj_u"
        w, wk = W(ti, l, 0)
        wv = w.rearrange("p (k c) -> p k c", k=8)
        for m in range(4):
            b = P.bank()
            for kt in range(8):
                mm(ps[:, b, :N], wv[:, kt, 128 * m:128 * m + 128], zT[:, kt, :N], kt == 0, kt == 7, [wk, zkeys[kt]], [PS(b)])
            P.op("act", lambda e, b=b, m=m: e.copy(uT[:, m, :N], ps[:, b, :N]), reads=[PS(b)], writes=[("uT", m)])
        P.tag = "s5_A"
        WB = [W(ti, l, 8), W(ti, l, 9, keep=1)]
        for c in range(4):
            for part in range(2):
                wbv = WB[part][0].rearrange("p (c i s) -> p c i s", c=4, i=8)
                bks = [P.bank() for _ in range(4)]
                for i in range(R):
                    for r in range(4):
                        kw = {"tile_position": (96, 0)} if r == 3 else {}
                        rhs = uT[32 * r:32 * r + 32, c, :N].rearrange("p (n s) -> p s n", s=R)[:, i, :]
                        mm(ps[:, bks[r], :nb], wbv[32 * r:32 * r + 32, c, i, :], rhs, i == 0, i == R - 1, [WB[part][1], ("uT", c)],
                           [PS(bks[r])], **kw)
                for r in range(4):
                    P.op("act", lambda e, be=bks[r], part=part, q=4 * c + r: e.copy(Gs[:, 1:nb + 1, part, q], ps[:, be, :nb]),
                         reads=[PS(bks[r])], writes=["Gs"])
        P.tag = "s5_scan"
        P.op("pool", lambda e: e.tensor_copy(Gs[:, 0, :, :], Gcar[:, l, :, :]), reads=["Gcar"], writes=["Gs"])
        for b_ in range(nb):
            prev = Gs[:, b_, :, :]
            prevsw = AP(Gs, (b_ * 32 + 16), [[65 * 32, 128], [-16, 2], [1, 16]])
            P.op("pool", lambda e, prev=prev: e.tensor_tensor(out=gt[0][:], in0=prev, in1=A1[:, l, :, :], op=ALU.mult),
                 reads=["Gs", "A1"], writes=["gt0"])
            P.op("pool", lambda e, prevsw=prevsw: e.tensor_tensor(out=gt[1][:], in0=prevsw, in1=A2[:, l, :, :], op=ALU.mult),
                 reads=["Gs", "A2"], writes=["gt1"])
            P.op("pool", lambda e: e.tensor_tensor(out=gt[0][:], in0=gt[0][:], in1=gt[1][:], op=ALU.add), reads=["gt0", "gt1"], writes=["gt0"])
            P.op("pool", lambda e, b_=b_: e.tensor_tensor(out=Gs[:, b_ + 1, :, :], in0=Gs[:, b_ + 1, :, :], in1=gt[0][:], op=ALU.add),
                 reads=["Gs", "gt0"], writes=["Gs"])
        P.op("pool", lambda e: e.tensor_copy(Gcar[:, l, :, :], Gs[:, nb, :, :]), reads=["Gs"], writes=["Gcar"])
        P.op("pool", lambda e: e.tensor_copy(Gbf[:, :, :, :nb].rearrange("p a q b -> p b (a q)"),
                                             Gs[:, 0:nb, :, :].rearrange("p b a q -> p b (a q)")), reads=["Gs"], writes=GBK)
        P.tag = "inproj_qk"
        w, wk = W(ti, l, 1)
        wv = w.rearrange("p (k c) -> p k c", k=8)
        for m in range(8):
            b = P.bank()
            for kt in range(8):
                mm(ps[:64, b, :N], wv[:, kt, 64 * m:64 * m + 64], zT[:, kt, :N], kt == 0, kt == 7, [wk, zkeys[kt]], [PS(b)])
            P.op("dve", lambda e, b=b, m=m: e.tensor_copy(qk[:, m, :N], ps[:64, b, :N]), reads=[PS(b)], writes=[("qk", m)])
        P.tag = "gla_gates"
        w, wk = W(ti, l, 3)
        wkt = w[:, 0:2048].rearrange("p (k c) -> p k c", k=8)
        wal = w[:, 2048:2176].rearrange("p (k c) -> p k c", k=8)
        b = P.bank()
        for kt in range(8):
            mm(ps[:16, b, :N], wal[:, kt, :], zT[:, kt, :N], kt == 0, kt == 7, [wk, zkeys[kt]], [PS(b)])
        P.op("dve", lambda e, b=b: e.tensor_copy(alT[0:16, :N], ps[:16, b, :N]), reads=[PS(b)], writes=["alT"])
        for ci, (c0, Lc) in enumerate(chunks):
            x = ci % 2
            bl = P.bank()
            mm(ps[:Lc, bl, :256], alT[0:17, c0:c0 + Lc], walp[0:17, l, :], True, True, ["alT", "walp"], [PS(bl)])
            P.op("act", lambda e, bl=bl, x=x, Lc=Lc: e.activation(out=etm[x][:Lc, :], in_=ps[:Lc, bl, :256], func=AF.Exp, scale=-1.0),
                 reads=[PS(bl)], writes=[("etm", x)])
            P.op("act", lambda e, x=x, Lc=Lc: e.activation(out=SPt[x][:Lc, :], in_=etm[x][:Lc, :], func=AF.Ln, bias=cst[:Lc, 1:2]),
                 reads=[("etm", x), "cst"], writes=[("SPt", x)])
            bb = P.bank()
            for h in range(4):
                mm(ps[:64, bb, 64 * h:64 * h + Lc], SPt[x][:Lc, 64 * h:64 * h + 64], maskU[:Lc, :Lc], True, True,
                   [("SPt", x), "maskU"], [PS(bb)])
            pv = ps[:64, bb, 0:256].rearrange("p (h t) -> p h t", h=4)[:, :, :Lc]
            P.op("act", lambda e, x=x, pv=pv, Lc=Lc: e.activation(out=E12[x][:, 0, :, :Lc], in_=pv, func=AF.Exp),
                 reads=[PS(bb)], writes=[("E12", x, 0)])
            P.op("act", lambda e, x=x, pv=pv, Lc=Lc: e.activation(out=E12[x][:, 1, :, :Lc], in_=pv, func=AF.Exp, scale=-1.0),
                 reads=[PS(bb)], writes=[("E12", x, 1)])
            P.op("dve", lambda e, x=x, c0=c0, Lc=Lc: e.scalar_tensor_tensor(out=qk[:, 0:4, c0:c0 + Lc], in0=qk[:, 0:4, c0:c0 + Lc],
                                                                            scalar=0.125, in1=E12[x][:, 0, :, :Lc], op0=ALU.mult,
                                                                            op1=ALU.mult),
                 reads=[("E12", x, 0)] + [("qk", m) for m in range(4)], writes=[("qk", m) for m in range(4)])
            P.op("dve", lambda e, x=x, c0=c0, Lc=Lc: e.tensor_tensor(out=qk[:, 4:8, c0:c0 + Lc], in0=qk[:, 4:8, c0:c0 + Lc],
                                                                     in1=E12[x][:, 1, :, :Lc], op=ALU.mult),
                 reads=[("E12", x, 1)] + [("qk", m) for m in range(4, 8)], writes=[("qk", m) for m in range(4, 8)])
            P.op("dve", lambda e, x=x, ci=ci, Lc=Lc: e.tensor_copy(dec[:, ci, :], E12[x][:, 0, :, Lc - 1]),
                 reads=[("E12", x, 0)], writes=[("dec", ci)])
            br = P.bank()
            mm(ps[:Lc, br, :256], maskL[:Lc, :Lc], SPt[x][:Lc, :], True, True, [("SPt", x), "maskL"], [PS(br)])
            P.op("act", lambda e, br=br, x=x, Lc=Lc: e.activation(out=E3t[x][:Lc, :], in_=ps[:Lc, br, :256], func=AF.Exp),
                 reads=[PS(br)], writes=[("E3t", x)])
            bk = P.bank()
            for kt in range(8):
                mm(ps[:Lc, bk, :256], zT[:, kt, c0:c0 + Lc], wkt[:, kt, :], kt == 0, kt == 7, [wk, zkeys[kt]], [PS(bk)])
            P.op("dve", lambda e, bk=bk, x=x, ci=ci, Lc=Lc: e.tensor_tensor(out=kst[:Lc, ci, :], in0=ps[:Lc, bk, :256], in1=E3t[x][:Lc, :],
                                                                            op=ALU.mult),
                 reads=[PS(bk), ("E3t", x)], writes=[("kst", ci)])
        P.tag = "v_tok"
        w, wk = W(ti, l, 4)
        wvv = w.rearrange("p (k c) -> p k c", k=8)
        for ci, (c0, Lc) in enumerate(chunks):
            bv = P.bank()
            for kt in range(8):
                mm(ps[:Lc, bv, :512], zT[:, kt, c0:c0 + Lc], wvv[:, kt, :], kt == 0, kt == 7, [wk, zkeys[kt]], [PS(bv)])
            P.op("act", lambda e, bv=bv, ci=ci, Lc=Lc: e.copy(vtok[:Lc, ci, :], ps[:Lc, bv, :512]), reads=[PS(bv)], writes=[("vtok", ci)])

        dump("uT", uT[:], [128, 4, 512], BF16, [("uT", m) for m in range(4)], ti, l)
        dump("qk", qk[:], [64, 8, 512], BF16, [("qk", m) for m in range(8)], ti, l)
        dump("kst", kst[:], [64, 8, 256], BF16, [("kst", m) for m in range(8)], ti, l)
        dump("vtok", vtok[:], [64, 8, 512], BF16, [("vtok", m) for m in range(8)], ti, l)
        dump("srT", srT[:], [128, 4, 512], BF16, [("srT", m) for m in range(4)], ti, l)
        dump("dec", dec[:], [64, 8, 4], F32, [("dec", m) for m in range(8)], ti, l)
        P.tag = "gla_core"
        Lc0 = chunks[0][1]
        for h in range(4):
            x = h % 2
            kvb = []
            for g0_ in range(0, nch, 4):
                bkv = P.bank()
                kvb.append(bkv)
                gl = min(4, nch - g0_)
                for ci in range(g0_, g0_ + gl):
                    c0, Lc = chunks[ci]
                    mm(ps[:64, bkv, 128 * (ci - g0_):128 * (ci - g0_) + 128], kst[:Lc, ci, 64 * h:64 * h + 64],
                       vtok[:Lc, ci, 128 * h:128 * h + 128], True, True, [("kst", ci), ("vtok", ci)], [PS(bkv)])
            P.op("dve", lambda e, x=x, h=h: e.tensor_copy(Sall[x][:, 0, :], Scar[:, l, h, :]), reads=[("Scar", l, h)], writes=[("Sall", x)])
            for ci in range(nch):
                bkv = kvb[ci // 4]
                P.op("dve", lambda e, x=x, ci=ci, h=h, bkv=bkv: e.scalar_tensor_tensor(
                    out=Sall[x][:, ci + 1, :], in0=Sall[x][:, ci, :], scalar=dec[:, ci, h:h + 1],
                    in1=ps[:64, bkv, 128 * (ci % 4):128 * (ci % 4) + 128], op0=ALU.mult, op1=ALU.add),
                     reads=[("Sall", x), ("dec", ci), PS(bkv)], writes=[("Sall", x)])
            P.op("dve", lambda e, x=x, h=h: e.tensor_copy(Scar[:, l, h, :], Sall[x][:, nch, :]), reads=[("Sall", x)], writes=[("Scar", l, h)])
            P.op("act", lambda e, x=x, h=h: e.copy(Sbf[h][:, :nch, :], Sall[x][:, :nch, :]), reads=[("Sall", x)], writes=[("Sbf", h)])
        for h in range(4):
            bs = P.bank()
            for ci, (c0, Lc) in enumerate(chunks):
                mm(ps[:Lc, bs, 64 * ci:64 * ci + Lc], qk[:, 4 + h, c0:c0 + Lc], qk[:, h, c0:c0 + Lc], True, True,
                   [("qk", 4 + h), ("qk", h)], [PS(bs)])
            pv = ps[:Lc0, bs, 0:64 * nch].rearrange("p (c t) -> p c t", t=64)[:, :, :Lc0]
            cm = AP(causal, 0, [[64, Lc0], [0, nch], [1, Lc0]])
            P.op("dve", lambda e, h=h, pv=pv, cm=cm: e.tensor_tensor(out=scm[h][:Lc0, :nch, :Lc0], in0=pv, in1=cm, op=ALU.mult),
                 reads=[PS(bs), "causal"], writes=[("scm", h)])
        P.tag = "inproj_r"
        w, wk = W(ti, l, 2)
        wv = w.rearrange("p (k c) -> p k c", k=8)
        for m in range(4):
            b = P.bank()
            for kt in range(8):
                mm(ps[:, b, :N], wv[:, kt, 128 * m:128 * m + 128], zT[:, kt, :N], kt == 0, kt == 7, [wk, zkeys[kt]], [PS(b)])
            P.op("act", lambda e, b=b, m=m: e.activation(out=srT[:, m, :N], in_=ps[:, b, :N], func=AF.Silu),
                 reads=[PS(b)], writes=[("srT", m)])
        P.tag = "gla_core"
        for h in range(4):
            bo = P.bank()
            for ci, (c0, Lc) in enumerate(chunks):
                mm(ps[:, bo, c0:c0 + Lc], vtok[:Lc, ci, 128 * h:128 * h + 128], scm[h][:Lc, ci, :Lc], True, False,
                   [("vtok", ci), ("scm", h)], [PS(bo)])
                mm(ps[:, bo, c0:c0 + Lc], Sbf[h][:, ci, :], qk[:, h, c0:c0 + Lc], False, True, [("Sbf", h), ("qk", h)], [PS(bo)])
            P.op("act", lambda e, bo=bo: e.activation(out=hid[:, 8, :N], in_=ps[:, bo, :N], func=AF.Square), reads=[PS(bo)], writes=[("hid", 8)])
            bm = P.bank()
            mm(ps[:, bm, :N], onesH[:], hid[:, 8, :N], True, True, [("hid", 8), "onesH"], [PS(bm)])
            P.op("act", lambda e, bm=bm: e.activation(out=rstd[:, :N], in_=ps[:, bm, :N], func=AF.Ln, bias=cst[:, 0:1]),
                 reads=[PS(bm), "cst"], writes=["rstd"])
            P.op("act", lambda e: e.activation(out=rstd[:, :N], in_=rstd[:, :N], func=AF.Exp, scale=-0.5), reads=["rstd"], writes=["rstd"])
            P.op("dve", lambda e, bo=bo, h=h: e.scalar_tensor_tensor(out=ftmp[0][:, :N], in0=ps[:, bo, :N], scalar=smalls[:, l, 24 + h:25 + h],
                                                                     in1=rstd[:, :N], op0=ALU.mult, op1=ALU.mult),
                 reads=[PS(bo), "rstd", "smalls"], writes=["ftmp0"])
            P.op("dve", lambda e, h=h: e.tensor_tensor(out=onT[:, h, :N], in0=ftmp[0][:, :N], in1=srT[:, h, :N], op=ALU.mult),
                 reads=["ftmp0", ("srT", h)], writes=[("onT", h)])

        dump("onT", onT[:], [128, 4, 512], BF16, [("onT", m) for m in range(4)], ti, l)
        P.tag = "yb"
        wpb, kpb = W(ti, l, 5)
        wpbv = wpb.rearrange("p (k c) -> p k c", k=4)
        wg = [W(ti, l, 6, keep=1), W(ti, l, 7, keep=2)]
        for m in range(8):
            wgv = wg[m // 4][0].rearrange("p (k c) -> p k c", k=8)
            wgk = wg[m // 4][1]
            bg = P.bank()
            for kt in range(8):
                mm(ps[:, bg, :N], wgv[:, kt, 128 * (m % 4):128 * (m % 4) + 128], zT[:, kt, :N], kt == 0, kt == 7, [wgk, zkeys[kt]], [PS(bg)])
            P.op("act", lambda e, bg=bg: e.activation(out=ftmp[1][:, :N], in_=ps[:, bg, :N], func=AF.Sigmoid), reads=[PS(bg)], writes=["ftmp1"])
            by = P.bank()
            for kt in range(4):
                mm(ps[:, by, :N], wpbv[:, kt, 128 * m:128 * m + 128], onT[:, kt, :N], kt == 0, kt == 3, [kpb, ("onT", kt)], [PS(by)])
            P.op("dve", lambda e, by=by, m=m: e.tensor_tensor(out=mixT[:, m, :N], in0=ps[:, by, :N], in1=ftmp[1][:, :N], op=ALU.mult),
                 reads=[PS(by), "ftmp1"], writes=[("mixT", m)])

        dump("mixb", mixT[:], [128, 8, 512], BF16, [("mixT", m) for m in range(8)], ti, l)
        dump("Gs", Gs[:], [128, 65, 2, 16], F32, ["Gs"], ti, l)
        P.tag = "s5_D"
        KTw, KTk = W(ti, l, 10)
        KTv = KTw.rearrange("p (c i s) -> p c i s", c=4, i=8)
        CA = [W(ti, l, 11, keep=1), W(ti, l, 12, keep=2)]
        for c in range(4):
            for j in range(R):
                by = P.bank()
                for i in range(j + 1):
                    rhs = uT[:, c, :N].rearrange("p (n s) -> p s n", s=R)[:, i, :]
                    mm(ps[:, by, :nb], KTv[:, c, j - i, :], rhs, i == 0, False, [KTk, ("uT", c)], [PS(by)])
                for part in range(2):
                    cav = CA[part][0].rearrange("p (q j s) -> p q j s", q=16, j=8)
                    for r in range(4):
                        kw = {"tile_position": (0, 96)} if r == 3 else {}
                        mm(ps[32 * r:32 * r + 32, by, :nb], cav[:, 4 * c + r, j, :], Gbf[:, part, 4 * c + r, :nb], False,
                           (r == 3 and part == 1), [CA[part][1]] + GBK, [PS(by)], **kw)
                uo = uT[:, c, :N].rearrange("p (n s) -> p s n", s=R)[:, j, :]
                yo = yT[:, c, :N].rearrange("p (n s) -> p s n", s=R)[:, j, :]
                P.op("dve", lambda e, by=by, uo=uo, yo=yo, c=c: e.scalar_tensor_tensor(out=yo, in0=uo, scalar=smalls[:, l, 16 + c:17 + c],
                                                                                       in1=ps[:, by, :nb], op0=ALU.mult, op1=ALU.add),
                     reads=[PS(by), ("uT", c), "smalls"], writes=[("yT", c)])
        for c in range(4):
            P.op("act", lambda e, c=c: e.activation(out=yT[:, c, :N], in_=yT[:, c, :N], func=AF.Gelu_apprx_tanh),
                 reads=[("yT", c)], writes=[("yT", c)])
            P.op("pool", lambda e, c=c: e.tensor_copy(hid[:, 9 + c, :N], yT[:, c, :N]), reads=[("yT", c)], writes=[("hid", 9 + c)])
        dump("act", yT[:], [128, 4, 512], F32, [("yT", m) for m in range(4)], ti, l)
        P.tag = "glu"
        wgl, kgl = W(ti, l, 13)
        wglv = wgl[:, 0:2048].rearrange("p (k c) -> p k c", k=4)
        for m in range(4):
            b = P.bank()
            for kt in range(4):
                mm(ps[:, b, :N], wglv[:, kt, 128 * m:128 * m + 128], hid[:, 9 + kt, :N], kt == 0, kt == 3, [kgl, ("hid", 9 + kt)], [PS(b)])
            P.op("act", lambda e, b=b, m=m: e.activation(out=ftmp[1][:, :N], in_=ps[:, b, :N], func=AF.Sigmoid,
                                                         bias=smalls[:, l, 20 + m:21 + m]), reads=[PS(b), "smalls"], writes=["ftmp1"])
            P.op("dve", lambda e, m=m: e.tensor_tensor(out=hid[:, 13 + m, :N], in0=yT[:, m, :N], in1=ftmp[1][:, :N], op=ALU.mult),
                 reads=[("yT", m), "ftmp1"], writes=[("hid", 13 + m)])
        dump("s5o", hid[:, 13:17, :], [128, 4, 512], BF16, [("hid", 13 + m) for m in range(4)], ti, l)
        P.tag = "ya"
        wpa, kpa = W(ti, l, 14)
        wpav = wpa.rearrange("p (k c) -> p k c", k=4)
        wg = [W(ti, l, 15, keep=1), W(ti, l, 16, keep=2)]
        for m in range(8):
            wgv = wg[m // 4][0].rearrange("p (k c) -> p k c", k=8)
            wgk = wg[m // 4][1]
            bg = P.bank()
            for kt in range(8):
                mm(ps[:, bg, :N], wgv[:, kt, 128 * (m % 4):128 * (m % 4) + 128], zT[:, kt, :N], kt == 0, kt == 7, [wgk, zkeys[kt]], [PS(bg)])
            P.op("act", lambda e, bg=bg: e.activation(out=ftmp[1][:, :N], in_=ps[:, bg, :N], func=AF.Sigmoid), reads=[PS(bg)], writes=["ftmp1"])
            by = P.bank()
            for kt in range(4):
                mm(ps[:, by, :N], wpav[:, kt, 128 * m:128 * m + 128], hid[:, 13 + kt, :N], kt == 0, kt == 3, [kpa, ("hid", 13 + kt)], [PS(by)])
            P.op("dve", lambda e, by=by: e.tensor_tensor(out=ftmp[0][:, :N], in0=ps[:, by, :N], in1=ftmp[1][:, :N], op=ALU.mult),
                 reads=[PS(by), "ftmp1"], writes=["ftmp0"])
            P.op("dve", lambda e, m=m: e.tensor_tensor(out=mixT[:, m, :N], in0=ftmp[0][:, :N], in1=mixT[:, m, :N], op=ALU.add),
                 reads=["ftmp0", ("mixT", m)], writes=[("mixT", m)])
        dump("mix", mixT[:], [128, 8, 512], BF16, [("mixT", m) for m in range(8)], ti, l)
        P.tag = "outproj"
        wo = [W(ti, l, 17), W(ti, l, 18, keep=1)]
        for m in range(8):
            wov = wo[m // 4][0].rearrange("p (k c) -> p k c", k=8)
            b = P.bank()
            for kt in range(8):
                mm(ps[:, b, :N], wov[:, kt, 128 * (m % 4):128 * (m % 4) + 128], mixT[:, kt, :N], kt == 0, kt == 7,
                   [wo[m // 4][1], ("mixT", kt)], [PS(b)])
            P.op("dve", lambda e, b=b, m=m: e.tensor_tensor(out=hT[:, m, :N], in0=hT[:, m, :N], in1=ps[:, b, :N], op=ALU.add),
                 reads=[PS(b), ("hT", m)], writes=[("hT", m)])
        dump("hmid", hT[:], [128, 8, 512], F32, [("hT", m) for m in range(8)], ti, l)
        P.tag = "ffn13"
        rmsnorm(N, lambda dt: smalls[:, l, 8 + dt:9 + dt], zT, "zT")
        for i in range(11):
            w, wk = W(ti, l, 19 + i)
            wv = w.rearrange("p (a k c) -> p a k c", a=2, k=8)
            for f2 in range(2):
                f = 2 * i + f2
                b1 = P.bank()
                for kt in range(8):
                    mm(ps[:, b1, :N], wv[:, 0, kt, 128 * f2:128 * f2 + 128], zT[:, kt, :N], kt == 0, kt == 7, [wk, zkeys[kt]], [PS(b1)])
                b3 = P.bank()
                for kt in range(8):
                    mm(ps[:, b3, :N], wv[:, 1, kt, 128 * f2:128 * f2 + 128], zT[:, kt, :N], kt == 0, kt == 7, [wk, zkeys[kt]], [PS(b3)])
                x = f % 2
                P.op("act", lambda e, b1=b1, x=x: e.activation(out=ftmp[x][:, :N], in_=ps[:, b1, :N], func=AF.Silu),
                     reads=[PS(b1)], writes=[f"ftmp{x}"])
                P.op("dve", lambda e, b3=b3, x=x, f=f: e.tensor_tensor(out=hid[:, f, :N], in0=ftmp[x][:, :N], in1=ps[:, b3, :N], op=ALU.mult),
                     reads=[PS(b3), f"ftmp{x}"], writes=[("hid", f)])
        P.tag = "ffn2"
        for m in range(8):
            w, wk = W(ti, l, 30 + m)
            wv = w[:, 0:22 * 128].rearrange("p (f c) -> p f c", f=22)
            b = P.bank()
            for f in range(22):
                mm(ps[:, b, :N], wv[:, f, :], hid[:, f, :N], f == 0, f == 21, [wk, ("hid", f)], [PS(b)])
            P.op("dve", lambda e, b=b, m=m: e.tensor_tensor(out=hT[:, m, :N], in0=hT[:, m, :N], in1=ps[:, b, :N], op=ALU.add),
                 reads=[PS(b), ("hT", m)], writes=[("hT", m)])
        dump("hend", hT[:], [128, 8, 512], F32, [("hT", m) for m in range(8)], ti, l)
        P.tag = "final"
        if l == L - 1 and ti > 0:
            rmsnorm(N, lambda dt: gf[:, dt:dt + 1], None, "yout", final_store=(t0 - NM))

    for ti in range(len(tiles)):
        for l in range(L):
            tile_layer(ti, l)
    P.barrier()
    P.emit()
    nc._prog = P
    return nc


_TILES_FULL = [(0, 16)] + [(16 + 512 * i, 512) for i in range(8)]


def make_in_map(inp, b, L, T):
    xT = np.empty((D, T), np.float32)
    xT[:, :NM] = inp["meta"].T
    xT[:, NM:] = inp["x"][b][:T - NM].T
    return {"xT": xT}


def shared_maps(inp, L):
    wpack = np.stack([pack_layer_weights(inp, l) for l in range(L)])
    s5 = [pack_s5(inp, l) for l in range(L)]
    sm, gf, walp = pack_small(inp, L)
    return {"wpack": wpack, "s5lam": np.stack([s[0] for s in s5]), "s5b": np.stack([s[1] for s in s5]),
            "s5c": np.stack([s[2] for s in s5]), "small": sm, "gfin": gf, "walp": walp}


def kernel(**inputs):
    inp = {k: np.asarray(v, dtype=np.float32) for k, v in inputs.items()}
    L = inp["w_in"].shape[0]
    B, SEQ, _ = inp["x"].shape
    T = SEQ + NM
    tiles = [(0, 16)] + [(16 + 512 * i, 512) for i in range(SEQ // 512)]
    import time as _t
    _t0 = _t.time()
    nc = build_nc(L, tiles)
    print("build_nc s", _t.time() - _t0, flush=True)
    shared = shared_maps(inp, L)
    in_maps = []
    for b in range(B):
        m = dict(shared)
        m.update(make_in_map(inp, b, L, T))
        in_maps.append(m)
    _t0 = _t.time()
    res = run_bass_kernel_spmd(nc, in_maps, core_ids=list(range(B)))
    print("run s", _t.time() - _t0, flush=True)
    out = np.empty((B, SEQ, D), np.float32)
    for b in range(B):
        out[b] = res.results[b]["outT"].T
    if DBG["on"]:
        DBG["res"] = {k: np.asarray(res.results[0][k]) for k in DBG["names"]}
    return out
```
